# Optimizing a Trainium2 kernel written in Bass

```python
import jax, jax.numpy as jnp
from jax import lax
import numpy as np

D_MODEL = 2048
BATCH = 4
SEQ = 4096
DEPTH = 1

N_BRANCH = 2
BRANCH_WIDTH = 2048
GMLP_WIDTH = BRANCH_WIDTH
GMLP_GROUPS = 8
GMLP_GROUP_DIM = GMLP_WIDTH // GMLP_GROUPS
GMLP_CHUNK = 128
SSD_WIDTH = BRANCH_WIDTH
SSD_HEAD_DIM = 64
SSD_HEADS = SSD_WIDTH // SSD_HEAD_DIM
SSD_GROUPS = 8
SSD_HEADS_PER_GROUP = SSD_HEADS // SSD_GROUPS
SSD_STATE = 128
SSD_CONV = 4
SSD_CHUNK = 128
SSD_CONV_DIM = SSD_WIDTH + 2 * SSD_GROUPS * SSD_STATE
SSD_NORM_GROUP = SSD_WIDTH // SSD_GROUPS
IN_PROJ_DIM = 3 * GMLP_WIDTH + SSD_WIDTH + SSD_CONV_DIM + SSD_HEADS + N_BRANCH * D_MODEL
EPS = 1e-5

kernel_name = "hybrid_gmlp_ssd_gated_merge"


def rms_norm(x, w):
    xf = x.astype(jnp.float32)
    y = xf * lax.rsqrt(jnp.mean(xf * xf, axis=-1, keepdims=True) + EPS)
    return (y * w.astype(jnp.float32)).astype(x.dtype)


def layer_norm(x, w, b):
    xf = x.astype(jnp.float32)
    mu = jnp.mean(xf, axis=-1, keepdims=True)
    var = jnp.mean(jnp.square(xf - mu), axis=-1, keepdims=True)
    y = (xf - mu) * lax.rsqrt(var + EPS)
    return (y * w.astype(jnp.float32) + b.astype(jnp.float32)).astype(x.dtype)


def gmlp_spatial_gating(u, v, z, ln_w, ln_b, w_s, b_s):
    b, s, _ = v.shape
    nc = s // GMLP_CHUNK
    v = layer_norm(v, ln_w, ln_b).reshape(b, nc, GMLP_CHUNK, GMLP_GROUPS, GMLP_GROUP_DIM)
    causal = jnp.tril(jnp.ones((GMLP_CHUNK, GMLP_CHUNK), dtype=bool))
    w_masked = jnp.where(causal[None], w_s, jnp.zeros_like(w_s))
    mixed = jnp.einsum('gts,bcsgd->bctgd', w_masked, v) + b_s.T[None, None, :, :, None]
    return jax.nn.silu(z) * u * mixed.reshape(b, s, GMLP_WIDTH)


def causal_depthwise_conv(u, w, bias):
    k_w = w.shape[0]
    s = u.shape[1]
    up = jnp.pad(u, ((0, 0), (k_w - 1, 0), (0, 0)))
    out = bias
    for k in range(k_w):
        out = out + up[:, k:k + s] * w[k]
    return out


def ssd_chunked_scan(xs, dt, a, bm, cm):
    b, s = xs.shape[:2]
    nc = s // SSD_CHUNK
    L, G, K, P, N = SSD_CHUNK, SSD_GROUPS, SSD_HEADS_PER_GROUP, SSD_HEAD_DIM, SSD_STATE
    x = (xs * dt[..., None]).reshape(b, nc, L, G, K, P)
    adt = (dt * a).reshape(b, nc, L, G, K).astype(jnp.float32)
    bm = bm.reshape(b, nc, L, G, N)
    cm = cm.reshape(b, nc, L, G, N)
    a_cum = jnp.cumsum(adt, axis=2)
    a_cum_t = jnp.moveaxis(a_cum, 2, -1)
    causal = jnp.tril(jnp.ones((L, L), dtype=bool))
    seg = a_cum_t[..., :, None] - a_cum_t[..., None, :]
    decay = jnp.where(causal, jnp.exp(jnp.where(causal, seg, jnp.zeros_like(seg))), jnp.zeros_like(seg))
    cb = jnp.einsum('bclgn,bcsgn->bcgls', cm, bm)
    scores = cb[:, :, :, None] * decay.astype(cb.dtype)
    y_diag = jnp.einsum('bcgkls,bcsgkp->bclgkp', scores, x)
    decay_to_end = jnp.exp(a_cum[:, :, -1:] - a_cum).astype(x.dtype)
    states = jnp.einsum('bclgn,bclgkp->bcgkpn', bm, x * decay_to_end[..., None])
    chunk_decay = jnp.exp(a_cum_t[..., -1]).astype(states.dtype)

    def step(h, inp):
        st, dec = inp
        return h * dec[..., None, None] + st, h

    h0 = jnp.zeros((b, G, K, P, N), dtype=states.dtype)
    _, prev = lax.scan(step, h0, (jnp.moveaxis(states, 1, 0), jnp.moveaxis(chunk_decay, 1, 0)))
    prev = jnp.moveaxis(prev, 0, 1)
    y_off = jnp.einsum('bclgn,bcgkpn->bclgkp', cm, prev) * jnp.exp(a_cum).astype(x.dtype)[..., None]
    return (y_diag + y_off).reshape(b, s, SSD_HEADS, P)


def ssd_branch(z, xbc, dt_raw, conv_w, conv_b, dt_bias, a_log, d_skip, norm_w):
    b, s, _ = xbc.shape
    xbc = jax.nn.silu(causal_depthwise_conv(xbc, conv_w, conv_b))
    gn = SSD_GROUPS * SSD_STATE
    xs = xbc[..., :SSD_WIDTH].reshape(b, s, SSD_HEADS, SSD_HEAD_DIM)
    bm = xbc[..., SSD_WIDTH:SSD_WIDTH + gn].reshape(b, s, SSD_GROUPS, SSD_STATE)
    cm = xbc[..., SSD_WIDTH + gn:].reshape(b, s, SSD_GROUPS, SSD_STATE)
    dt = jax.nn.softplus(dt_raw + dt_bias)
    a = -jnp.exp(a_log)
    y = ssd_chunked_scan(xs, dt, a, bm, cm) + d_skip[:, None] * xs
    y = y.reshape(b, s, SSD_WIDTH) * jax.nn.silu(z)
    yf = y.astype(jnp.float32).reshape(b, s, SSD_GROUPS, SSD_NORM_GROUP)
    yf = yf * lax.rsqrt(jnp.mean(yf * yf, axis=-1, keepdims=True) + EPS)
    return (yf.reshape(b, s, SSD_WIDTH) * norm_w.astype(jnp.float32)).astype(y.dtype)


def setup_inputs(seed: int = 0) -> dict:
    key = jax.random.key(seed)
    ks = jax.random.split(key, 17)
    nrm = jax.random.normal
    x = nrm(ks[0], (BATCH, SEQ, D_MODEL), jnp.float32)
    norm_w = 1.0 + 0.02 * nrm(ks[1], (DEPTH, D_MODEL), jnp.float32)
    w_in = nrm(ks[2], (DEPTH, D_MODEL, IN_PROJ_DIM), jnp.float32) * D_MODEL ** -0.5
    b_gate = 0.02 * nrm(ks[3], (DEPTH, N_BRANCH, D_MODEL), jnp.float32)
    ln_v_w = 1.0 + 0.02 * nrm(ks[4], (DEPTH, GMLP_WIDTH), jnp.float32)
    ln_v_b = 0.02 * nrm(ks[5], (DEPTH, GMLP_WIDTH), jnp.float32)
    w_spatial = nrm(ks[6], (DEPTH, GMLP_GROUPS, GMLP_CHUNK, GMLP_CHUNK), jnp.float32) * GMLP_CHUNK ** -0.5
    b_spatial = 1.0 + 0.1 * nrm(ks[7], (DEPTH, GMLP_GROUPS, GMLP_CHUNK), jnp.float32)
    conv_w = jax.random.uniform(ks[8], (DEPTH, SSD_CONV, SSD_CONV_DIM), jnp.float32, -1.0, 1.0) * SSD_CONV ** -0.5
    conv_b = 0.02 * nrm(ks[9], (DEPTH, SSD_CONV_DIM), jnp.float32)
    dt0 = jnp.exp(jax.random.uniform(ks[10], (DEPTH, SSD_HEADS), jnp.float32, np.log(1e-3), np.log(1e-1)))
    dt_bias = dt0 + jnp.log(-jnp.expm1(-dt0))
    a_log = jnp.log(jax.random.uniform(ks[11], (DEPTH, SSD_HEADS), jnp.float32, 1.0, 16.0))
    d_skip = 1.0 + 0.1 * nrm(ks[12], (DEPTH, SSD_HEADS), jnp.float32)
    ssm_norm_w = 1.0 + 0.02 * nrm(ks[13], (DEPTH, SSD_WIDTH), jnp.float32)
    w_branch = nrm(ks[14], (DEPTH, N_BRANCH, BRANCH_WIDTH, D_MODEL), jnp.float32) * BRANCH_WIDTH ** -0.5
    w_out = nrm(ks[15], (DEPTH, D_MODEL, D_MODEL), jnp.float32) * D_MODEL ** -0.5
    final_norm_w = 1.0 + 0.02 * nrm(ks[16], (D_MODEL,), jnp.float32)
    return {"x": x, "norm_w": norm_w, "w_in": w_in, "b_gate": b_gate, "ln_v_w": ln_v_w, "ln_v_b": ln_v_b,
            "w_spatial": w_spatial, "b_spatial": b_spatial, "conv_w": conv_w, "conv_b": conv_b,
            "dt_bias": dt_bias, "a_log": a_log, "d_skip": d_skip, "ssm_norm_w": ssm_norm_w,
            "w_branch": w_branch, "w_out": w_out, "final_norm_w": final_norm_w}


def reference(x, norm_w, w_in, b_gate, ln_v_w, ln_v_b, w_spatial, b_spatial, conv_w, conv_b,
              dt_bias, a_log, d_skip, ssm_norm_w, w_branch, w_out, final_norm_w):
    b, s, _ = x.shape
    o1 = GMLP_WIDTH
    o2 = 2 * GMLP_WIDTH
    o3 = 3 * GMLP_WIDTH
    o4 = o3 + SSD_WIDTH
    o5 = o4 + SSD_CONV_DIM
    o6 = o5 + SSD_HEADS
    for l in range(DEPTH):
        h = rms_norm(x, norm_w[l])
        proj = h @ w_in[l]
        u, v, z_a = proj[..., :o1], proj[..., o1:o2], proj[..., o2:o3]
        z_b, xbc, dt_raw = proj[..., o3:o4], proj[..., o4:o5], proj[..., o5:o6]
        gate_logits = proj[..., o6:].reshape(b, s, N_BRANCH, D_MODEL)
        y_a = gmlp_spatial_gating(u, v, z_a, ln_v_w[l], ln_v_b[l], w_spatial[l], b_spatial[l])
        y_b = ssd_branch(z_b, xbc, dt_raw, conv_w[l], conv_b[l], dt_bias[l], a_log[l], d_skip[l], ssm_norm_w[l])
        branches = jnp.stack([y_a, y_b], axis=2)
        branch_d = jnp.einsum('bsne,ned->bsnd', branches, w_branch[l])
        gates = jax.nn.sigmoid(gate_logits + b_gate[l])
        merged = jnp.sum(gates * branch_d, axis=2)
        x = x + merged @ w_out[l]
    return rms_norm(x, final_norm_w)
```

```python
import numpy as np
from contextlib import ExitStack
import concourse.bass as bass
import concourse.mybir as mybir
from concourse.bass_utils import run_bass_kernel_spmd

F32 = mybir.dt.float32
BF16 = mybir.dt.bfloat16
AF = mybir.ActivationFunctionType
ALU = mybir.AluOpType
AX = mybir.AxisListType

D = 2048
E = 2048
O1, O2, O3, O4 = 2048, 4096, 6144, 8192
OB = O4 + 2048
OC = OB + 1024
O5 = O4 + 4096
O6 = O5 + 32
EPS = 1e-5
NSLOT = 8
T = 512
NEG = -30000.0

C_NW, C_LNW, C_LNB, C_CW, C_CB, C_SNW, C_BG, C_FW, C_DS, C_FLAG = 0, 16, 32, 48, 176, 208, 224, 256, 272, 288
NCOL = 289


class Tok:
    __slots__ = ("key", "count")

    def __init__(self, key, count):
        self.key = key
        self.count = count


class Buf:
    __slots__ = ("w", "r", "excl")

    def __init__(self, excl=False):
        self.w = []
        self.r = []
        self.excl = excl


class _Rec:
    def __init__(self):
        self.call = None

    def __getattr__(self, name):
        def f(*a, **k):
            self.call = (name, a, k)
            return self
        return f


def _free(ap):
    n = 1
    for d in ap.shape[1:]:
        n *= int(d)
    return n


def _record(fn):
    r = _Rec()
    fn(r)
    name, a, k = r.call
    dur = 0.3
    tbl = None
    if name == "activation":
        tbl = _TBL.get(str(k.get("func")), None)
    try:
        if name == "matmul":
            rhs = k.get("rhs", a[2] if len(a) > 2 else None)
            dur = 0.03 + _free(rhs) * 0.00043
        elif name == "transpose":
            dur = 0.09
        elif name == "dma_start":
            o = k["out"]
            dur = 2.0 + 128 * _free(o) * 4 / 180e3
        elif name in ("activation", "copy"):
            dur = 0.22 + _free(k["out"]) * 0.00085
        elif name == "memset" or name == "affine_select":
            dur = 0.5
        else:
            dur = 0.12 + _free(k["out"]) * 0.00115
    except Exception:
        pass
    return (lambda e: getattr(e, name)(*a, **k)), dur, tbl


_TBL = {str(AF.Exp): "exp", str(AF.Ln): "ln", str(AF.Silu): "silu", str(AF.Sigmoid): "sigm", str(AF.Sqrt): "sqrt"}


class Node:
    __slots__ = ("id", "eng", "fns", "dur", "deps", "dma_key", "start", "finish", "count", "tbl")

    def __init__(self, nid, eng, fns, dur, deps, dma_key, tbl=None):
        self.tbl = tbl
        self.id = nid
        self.eng = eng
        self.fns = fns
        self.dur = dur
        self.deps = deps
        self.dma_key = dma_key
        self.start = None
        self.finish = None
        self.count = None


class FW:
    ENGS = ("pe", "act", "dve", "pool", "sp")
    WINDOW = 160
    LAT = 0.1

    def __init__(self, nc, es):
        self.nc = nc
        self.es = es
        self.nodes = []
        self.sems = {}
        self.finals = []
        for e in self.ENGS:
            self.newsem(e)

    def newsem(self, key):
        self.sems[key] = self.es.enter_context(self.nc.semaphore("s_" + str(key)))
        return key

    @staticmethod
    def _split(reads, writes):
        writes = list(writes) + [b for b in reads if b.excl]
        reads = [b for b in reads if not b.excl]
        return reads, writes

    def _add(self, eng, fns, dur, reads, writes, dma_key, tbl=None):
        reads, writes = self._split(reads, writes)
        deps = set()
        for b in reads:
            deps.update(b.w)
        for b in writes:
            deps.update(b.w)
            deps.update(b.r)
        nid = len(self.nodes)
        self.nodes.append(Node(nid, eng, fns, dur, deps, dma_key, tbl))
        for b in reads:
            b.r.append(nid)
        for b in writes:
            b.w = [nid]
            b.r = []
        return nid

    def issue(self, eng, fn, reads=(), writes=(), dma_key=None):
        f, dur, tbl = _record(fn)
        return self._add(eng, [f], dur, reads, writes, dma_key, tbl)

    def group(self, eng, fns, reads=(), writes=()):
        rec = [_record(fn) for fn in fns]
        return self._add(eng, [r[0] for r in rec], sum(r[1] for r in rec), reads, writes, None)

    def final_wait(self, eng, key):
        self.finals.append((eng, key))

    def _schedule(self):
        nodes = self.nodes
        queues = {e: [n for n in nodes if n.eng == e] for e in self.ENGS}
        head = {e: 0 for e in self.ENGS}
        free = {e: 0.0 for e in self.ENGS}
        order = {e: [] for e in self.ENGS}
        done = [False] * len(nodes)
        remaining = len(nodes)
        cur_tbl = None
        TLOAD = 1.3
        while remaining:
            best = None
            for e in self.ENGS:
                q = queues[e]
                h = head[e]
                while h < len(q) and done[q[h].id]:
                    h += 1
                head[e] = h
                cnt = 0
                i = h
                while i < len(q) and cnt < self.WINDOW:
                    n = q[i]
                    i += 1
                    if done[n.id]:
                        continue
                    cnt += 1
                    ready = 0.0
                    ok = True
                    for d in n.deps:
                        if not done[d]:
                            ok = False
                            break
                        f = nodes[d].finish + (0.0 if nodes[d].eng == e == "pe" else self.LAT)
                        if f > ready:
                            ready = f
                    if not ok:
                        continue
                    st = max(free[e], ready)
                    pen = TLOAD if (n.tbl is not None and n.tbl != cur_tbl) else 0.0
                    key = (st + pen, n.id)
                    if best is None or key < best[0]:
                        best = (key, e, n)
                    if ready <= free[e] and pen == 0.0:
                        break
            (st, _), e, n = best
            if n.tbl is not None:
                cur_tbl = n.tbl
            if n.dma_key is not None:
                n.finish = st + n.dur
                free[e] = st + 0.15
            else:
                n.finish = st + n.dur
                free[e] = n.finish
            done[n.id] = True
            order[e].append(n)
            remaining -= 1
        return order

    def emit(self):
        order = self._schedule()
        nodes = self.nodes
        cnt = {}
        for e in self.ENGS:
            for n in order[e]:
                key = n.dma_key if n.dma_key is not None else e
                inc = 16 if n.dma_key is not None else 1
                cnt[key] = cnt.get(key, 0) + inc
                n.count = (key, cnt[key])
        ops = {e: [] for e in self.ENGS}
        for e in self.ENGS:
            waited = {}
            for n in order[e]:
                need = {}
                for d in n.deps:
                    dn = nodes[d]
                    key, c = dn.count
                    if e == "pe" and key == "pe":
                        continue
                    if c > need.get(key, 0):
                        need[key] = c
                for key, c in need.items():
                    if waited.get(key, 0) >= c:
                        continue
                    waited[key] = c
                    sem = self.sems[key]
                    ops[e].append(lambda eng, sem=sem, c=c: eng.wait_ge(sem, c))
                for f in n.fns[:-1]:
                    ops[e].append(f)
                key, c = n.count
                sem = self.sems[key]
                inc = 16 if n.dma_key is not None else 1
                last = n.fns[-1]
                ops[e].append(lambda eng, last=last, sem=sem, inc=inc: last(eng).then_inc(sem, inc))
            for (fe, key) in self.finals:
                if fe == e and key in cnt:
                    sem = self.sems[key]
                    c = cnt[key]
                    ops[e].append(lambda eng, sem=sem, c=c: eng.wait_ge(sem, c))
        with self.nc.Block() as block:
            @block.tensor
            def _(e):
                for f in ops["pe"]:
                    f(e)

            @block.scalar
            def _(e):
                for f in ops["act"]:
                    f(e)

            @block.vector
            def _(e):
                for f in ops["dve"]:
                    f(e)

            @block.gpsimd
            def _(e):
                for f in ops["pool"]:
                    f(e)

            @block.sync
            def _(e):
                for f in ops["sp"]:
                    f(e)


def ssd_blk(g, k):
    return g * 6 + k


def gm_blk(g, k):
    return 48 + g * 4 + k


def br_blk(d, k):
    return 80 + d * 4 + k


def out_blk(d):
    return 144 + d


NBLK = 160


def stream_plan(items):
    pos = 0
    plan = []
    for kind, idx in items:
        if kind == "w":
            pos = (pos + 3) // 4 * 4
            plan.append((kind, idx, pos, 4))
            pos += 4
        else:
            plan.append((kind, idx, pos, 1))
            pos += 1
    return plan


def _old_stream_items():
    items = []
    for ps in range(8):
        if ps < 4:
            for g in range(8):
                items.append(("n", ssd_blk(g, 2)))
                items.append(("n", ssd_blk(g, 3)))
                items.append(("n", ssd_blk(g, 4)))
                if ps == 3:
                    items.append(("n", ssd_blk(g, 5)))
        else:
            for g in range(8):
                for k in range(6):
                    items.append(("n", ssd_blk(g, k)))
            for j in range(4):
                items.append(("w", j))
            for g in range(8):
                for k in range(4):
                    items.append(("n", gm_blk(g, k)))
            for d in range(16):
                for k in range(4):
                    items.append(("n", br_blk(d, k)))
            for d in range(16):
                items.append(("n", out_blk(d)))
    pos = 0
    plan = []
    for kind, idx in items:
        if kind == "w":
            pos = (pos + 3) // 4 * 4
            plan.append((kind, idx, pos, 4))
            pos += 4
        else:
            plan.append((kind, idx, pos, 1))
            pos += 1
    return plan


class _Stop(Exception):
    pass


DEBUG_STOP = None


def _cp(k):
    if DEBUG_STOP is not None and DEBUG_STOP == k:
        raise _Stop()


def build_program():
    order = _build(None)
    return _build(stream_plan(order))


def _build(plan):
    dry = plan is None
    order = []
    nc = bass.Bass("TRN2", target_bir_lowering=False)
    x_all = nc.dram_tensor("x_all", [8, 128, 8192], F32, kind="ExternalInput").ap()
    wn = nc.dram_tensor("wn", [NBLK, 128, 2048], F32, kind="ExternalInput").ap()
    wv = nc.dram_tensor("wv", [4, 128, 8192], F32, kind="ExternalInput").ap()
    wdt_d = nc.dram_tensor("wdt", [128, 512], F32, kind="ExternalInput").ap()
    cols_d = nc.dram_tensor("cols", [128, NCOL], F32, kind="ExternalInput").ap()
    rows_d = nc.dram_tensor("rows", [128, 64], F32, kind="ExternalInput").ap()
    bsb_d = nc.dram_tensor("bsb", [128, 1024], F32, kind="ExternalInput").ap()
    wst_d = nc.dram_tensor("wst", [128, 1024], F32, kind="ExternalInput").ap()
    out_d = nc.dram_tensor("out", [4, 128, 8192], F32, kind="ExternalOutput").ap()

    with ExitStack() as es:
        fw = FW(nc, es)

        def sb(name, shape, dt):
            return es.enter_context(nc.sbuf_tensor("sb_" + name, shape, dt))

        ring = sb("ring", [128, NSLOT * 2048], BF16)
        B_ring = [Buf() for _ in range(NSLOT)]
        hT = sb("hT", [128, 16, 512], BF16)
        B_hT = [Buf() for _ in range(16)]
        vm = sb("vm", [128, 8192], BF16)
        vn = vm[:].rearrange("p (c n) -> p c n", c=4)
        mg = vm[:].rearrange("p (k t) -> p k t", k=16)
        B_vm = [Buf() for _ in range(4)]
        ybuf = sb("ybuf", [128, 8192], F32)
        yT = ybuf[:].bitcast(BF16).rearrange("p (b t) -> p b t", t=512)
        xnew = ybuf[:].rearrange("p (k t) -> p k t", k=16)
        B_y = [Buf() for _ in range(32)]
        cols = sb("cols", [128, NCOL], F32)
        B_cols = Buf()
        rows = sb("rows", [128, 64], F32)
        B_rows = Buf()
        ident = sb("ident", [128, 128], BF16)
        tri = sb("tri", [128, 128], BF16)
        ones = sb("ones", [128, 128], BF16)
        maskneg = sb("maskneg", [128, 128], F32)
        sel = sb("sel", [128, 4, 128], BF16)
        WgT = sb("WgT", [128, 8, 128], BF16)
        Kc = sb("Kc", [128, 16, 128], F32)
        wdt = sb("wdt", [128, 16, 32], BF16)
        negA = sb("negA", [128, 32], F32)
        B_const = Buf()
        H = sb("H", [128, 8, 256], F32)
        B_H = [Buf() for _ in range(8)]
        Hpad = sb("Hpad", [128, 2, 4, 4, 128], BF16)
        B_Hpad = [[Buf() for _ in range(4)] for _ in range(2)]
        halo = sb("halo", [128, 32, 3], F32)
        B_halo = [Buf() for _ in range(32)]
        xt = sb("xt", [128, 3, 512], F32)
        B_xt = [Buf() for _ in range(3)]
        sq = sb("sq", [128, 2, 512], BF16)
        B_sq = [Buf() for _ in range(2)]
        sqp = sb("sqp", [128, 2, 512], BF16)
        B_sqp = [Buf() for _ in range(2)]
        xTg = sb("xTg", [128, 2, 2, 512], BF16)
        B_xTg = [[Buf() for _ in range(2)] for _ in range(2)]
        BTg = sb("BTg", [128, 2, 512], BF16)
        B_BTg = [Buf() for _ in range(2)]
        CTg = sb("CTg", [128, 2, 512], BF16)
        B_CTg = [Buf() for _ in range(2)]
        szb = sb("szb", [128, 2, 2, 512], BF16)
        B_szb = [[Buf() for _ in range(2)] for _ in range(2)]
        xpre = sb("xpre", [128, 3, 515], F32)
        B_xpre = [Buf() for _ in range(3)]
        btok = sb("btok", [128, 4, 128], BF16)
        B_btok = [Buf() for _ in range(4)]
        xdtpad = sb("xdtpad", [128, 4, 4, 128], BF16)
        B_xdt = [Buf() for _ in range(4)]
        xdec = sb("xdec", [128, 4, 256], BF16)
        B_xdec = [Buf() for _ in range(4)]
        scT = sb("scT", [128, 4, 512], BF16)
        B_scT = [Buf() for _ in range(4)]
        CE = sb("CE", [128, 4, 512], BF16)
        B_CE = [Buf() for _ in range(4)]
        acT = sb("acT", [128, 2, 512], BF16)
        B_acT = Buf()
        NT32 = 15
        t32 = sb("t32", [128, NT32, 512], F32)
        B_t32 = [Buf() for _ in range(NT32)]
        dtt = sb("dtt", [128, 4, 32], F32)
        adt = sb("adt", [128, 4, 32], F32)
        adt_hi = sb("adt_hi", [128, 4, 32], BF16)
        adt_lo = sb("adt_lo", [128, 4, 32], BF16)
        acum = sb("acum", [128, 4, 32], F32)
        aend = sb("aend", [128, 4, 32], F32)
        w2 = sb("w2", [128, 4, 32], F32)
        edec = sb("edec", [128, 4, 32], F32)
        dtmp = sb("dtmp", [128, 4, 32], F32)
        B_dt = Buf()
        B_dtw = Buf()
        B_adt = Buf()
        st1 = sb("st1", [128, 4, 4], F32)
        st2 = sb("st2", [128, 4, 4], F32)
        stm = sb("stm", [128, 8, 4], F32)
        B_st = Buf()

        pb = [es.enter_context(nc.psum_tensor("pb%d" % i, [128, 512], F32)) if i != 3 else None for i in range(8)]
        p3t = es.enter_context(nc.psum_tensor("p3t", [128, 512], BF16))
        B_pb = [Buf(excl=True) for _ in range(8)]

        class Rot:
            def __init__(self, lst):
                self.lst = lst
                self.i = 0

            def next(self):
                v = self.lst[self.i % len(self.lst)]
                self.i += 1
                return v

        wstate = {"next_issue": 0, "next_use": 0}
        k_ring = [fw.newsem("ring%d" % i) for i in range(NSLOT)]

        def ws_issue_upto(limit_pos):
            while wstate["next_issue"] < len(plan):
                kind, idx, pos, ln = plan[wstate["next_issue"]]
                if pos + ln > limit_pos:
                    break
                s0 = pos % NSLOT
                bufs = [B_ring[s0 + i] for i in range(ln)]
                dst = ring[:, s0 * 2048:(s0 + ln) * 2048]
                src = wn[idx] if kind == "n" else wv[idx]
                fw.issue("pool", lambda e, dst=dst, src=src: e.dma_start(out=dst, in_=src),
                         writes=bufs, dma_key=k_ring[s0])
                wstate["next_issue"] += 1

        def ws_next(kind, idx):
            if dry:
                order.append((kind, idx))
                ln = 4 if kind == "w" else 1
                return ring[:, 0:ln * 2048].rearrange("p (k c) -> p k c", k=16), [B_ring[i] for i in range(ln)]
            k, i, pos, ln = plan[wstate["next_use"]]
            assert (k, i) == (kind, idx), ((k, i), (kind, idx))
            wstate["next_use"] += 1
            ws_issue_upto(pos + NSLOT)
            s0 = pos % NSLOT
            bufs = [B_ring[s0 + j] for j in range(ln)]
            if kind == "n":
                view = ring[:, s0 * 2048:(s0 + 1) * 2048].rearrange("p (k c) -> p k c", k=16)
            else:
                view = ring[:, s0 * 2048:(s0 + 4) * 2048].rearrange("p (k c) -> p k c", k=16)
            return view, bufs

        k_c = fw.newsem("kconst0")
        k_c1 = fw.newsem("kconst1")
        k_c2 = fw.newsem("kconst2")
        k_c3 = fw.newsem("kconst3")
        fw.issue("sp", lambda e: e.dma_start(out=cols[:], in_=cols_d[:, :]), writes=[B_cols], dma_key=k_c)
        fw.issue("sp", lambda e: e.dma_start(out=rows[:], in_=rows_d[:, :]), writes=[B_rows], dma_key=k_c1)
        rsW = t32[:, 4:6, :].rearrange("p a (g t) -> p (a g) t", g=4)
        bs_bc = t32[:, 0:2, :].rearrange("p a (g t) -> p (a g) t", g=4)
        ws32 = t32[:, 2:4, :].rearrange("p a (g t) -> p (a g) t", g=4)
        fw.issue("sp", lambda e: e.dma_start(out=t32[:, 0:2, :].rearrange("p a t -> p (a t)"), in_=bsb_d[:, :]),
                 writes=[B_t32[0], B_t32[1]], dma_key=k_c2)
        fw.issue("sp", lambda e: e.dma_start(out=t32[:, 2:4, :].rearrange("p a t -> p (a t)"), in_=wst_d[:, :]),
                 writes=[B_t32[2], B_t32[3]], dma_key=k_c3)
        k_wdt = fw.newsem("kwdt")
        B_wdt = Buf()
        fw.issue("pool", lambda e: e.dma_start(out=wdt[:].rearrange("p k c -> p (k c)"), in_=wdt_d[:, :]),
                 writes=[B_wdt], dma_key=k_wdt)
        fw.issue("pool", lambda e: e.memset(ident[:], 1.0), writes=[B_const])
        fw.issue("pool", lambda e: e.affine_select(out=ident[:], in_=ident[:], pattern=[[-1, 128]],
                                                   compare_op=ALU.is_equal, fill=0.0, base=0, channel_multiplier=1),
                 writes=[B_const])
        fw.issue("pool", lambda e: e.memset(tri[:], 1.0), writes=[B_const])
        fw.issue("pool", lambda e: e.affine_select(out=tri[:], in_=tri[:], pattern=[[1, 128]],
                                                   compare_op=ALU.is_ge, fill=0.0, base=0, channel_multiplier=-1),
                 writes=[B_const])
        fw.issue("pool", lambda e: e.memset(maskneg[:], 0.0), writes=[B_const])
        fw.issue("pool", lambda e: e.affine_select(out=maskneg[:], in_=maskneg[:], pattern=[[1, 128]],
                                                   compare_op=ALU.is_ge, fill=NEG, base=0, channel_multiplier=-1),
                 writes=[B_const])
        fw.issue("pool", lambda e: e.memset(ones[:], 1.0), writes=[B_const])
        fw.issue("pool", lambda e: e.memset(sel[:], 1.0), writes=[B_const])
        for hl in range(4):
            fw.issue("pool", lambda e, hl=hl: e.affine_select(out=sel[:, hl, :], in_=sel[:, hl, :], pattern=[[0, 128]],
                                                              compare_op=ALU.is_equal, fill=0.0, base=-hl,
                                                              channel_multiplier=1), writes=[B_const])
        fw.issue("pool", lambda e: e.memset(H[:], 0.0), writes=B_H)
        fw.issue("pool", lambda e: e.memset(Hpad[:], 0.0), writes=B_Hpad[0] + B_Hpad[1])
        fw.issue("pool", lambda e: e.memset(halo[:], 0.0), writes=B_halo)
        fw.issue("pool", lambda e: e.memset(xdtpad[:], 0.0), writes=B_xdt)
        fw.issue("pool", lambda e: e.memset(acT[:], 0.0), writes=[B_acT])
        fw.issue("dve", lambda e: e.tensor_copy(out=WgT[:], in_=ws32), reads=[B_t32[2], B_t32[3]], writes=[B_const])
        for g in range(8):
            fw.issue("pool", lambda e, g=g: e.affine_select(out=WgT[:, g, :], in_=WgT[:, g, :], pattern=[[1, 128]],
                                                            compare_op=ALU.is_ge, fill=0.0, base=0,
                                                            channel_multiplier=-1), writes=[B_const])
        for half in range(2):
            fw.group("pe", [lambda e, g=g, half=half: e.matmul(pb[half][:, (g % 4) * 128:(g % 4 + 1) * 128], lhsT=ones[:],
                                                              rhs=WgT[:, g, :], start=True, stop=True)
                            for g in range(half * 4, half * 4 + 4)], reads=[B_const], writes=[B_pb[half]])
            fw.issue("act", lambda e, half=half: e.copy(out=t32[:, 4 + half, :], in_=pb[half][:]), reads=[B_pb[half]], writes=[B_t32[4 + half]])
        for cb in range(16):
            fw.issue("dve", lambda e, cb=cb: e.scalar_tensor_tensor(out=Kc[:, cb, :], in0=rsW[:, cb // 2, :],
                                                                    scalar=cols[:, C_LNB + cb:C_LNB + cb + 1],
                                                                    in1=bs_bc[:, cb // 2, :], op0=ALU.mult, op1=ALU.add),
                     reads=[B_const, B_cols, B_t32[0], B_t32[1], B_t32[4], B_t32[5]], writes=[B_const])
        fw.issue("act", lambda e: e.activation(out=negA[:], in_=rows[:, 32:64], func=AF.Exp), reads=[B_rows], writes=[B_const])
        fw.issue("dve", lambda e: e.tensor_scalar(out=negA[:], in0=negA[:], scalar1=-1.0, scalar2=None, op0=ALU.mult),
                 reads=[B_const], writes=[B_const])

        k_xt = [fw.newsem("kxt%d" % i) for i in range(3)]
        xt_rot = Rot([0, 1, 2])
        sq_rot = Rot([0, 1])
        sqp_rot = Rot([0, 1])
        k_out = {to: fw.newsem("kout%d" % to) for to in (7, 8, 9)}

        def load_x(ps, kc):
            i = xt_rot.next()
            fw.issue("sp", lambda e: e.dma_start(out=xt[:, i, :], in_=x_all[ps][:, kc * 512:(kc + 1) * 512]),
                     writes=[B_xt[i]], dma_key=k_xt[i])
            return i

        def rstd_from_psum(bank, scale, tdst):
            fw.issue("act", lambda e: e.activation(out=t32[:, tdst, :], in_=pb[bank][:], func=AF.Sqrt, bias=EPS, scale=scale),
                     reads=[B_pb[bank]], writes=[B_t32[tdst]])
            fw.issue("dve", lambda e: e.reciprocal(out=t32[:, tdst, :], in_=t32[:, tdst, :]),
                     reads=[B_t32[tdst]], writes=[B_t32[tdst]])

        hT_alt = vm[:].rearrange("p (k t) -> p k t", k=16)
        hbufs = [hT[:], hT_alt]
        B_hTs = [B_hT, [B_vm[kc // 4] for kc in range(16)]]
        cur = {"h": hbufs[0], "B_h": B_hTs[0]}

        def hsel(ps):
            return ps % 2 if ps < 4 else 0

        def inproj(view, wbufs, bank, hi=None):
            hc = cur["h"] if hi is None else hbufs[hi]
            Bh = cur["B_h"] if hi is None else B_hTs[hi]
            fw.group("pe", [lambda e, kc=kc: e.matmul(pb[bank][:], lhsT=view[:, kc, :], rhs=hc[:, kc, :],
                                                      start=(kc == 0), stop=(kc == 15)) for kc in range(16)],
                     reads=list(wbufs) + list(dict.fromkeys(Bh)), writes=[B_pb[bank]])

        rot = Rot([0, 1, 2])
        rot_main = rot
        rot_pre = Rot([0, 1, 2, 5, 6])
        a0_done = set()

        def A_units(ps_, g):
            main_ = ps_ >= 4
            last_pre_ = ps_ == 3
            hi = hsel(ps_)
            par = g % 2
            rot = rot_pre if ps_ < 4 else rot_main

            def a_z(j):
                view, wb = ws_next("n", ssd_blk(g, j))
                bank = rot.next()
                inproj(view, wb, bank, hi)
                fw.issue("act", lambda e: e.activation(out=szb[:, par, j, :], in_=pb[bank][:], func=AF.Silu),
                         reads=[B_pb[bank]], writes=[B_szb[par][j]])

            def a_x(j):
                view, wb = ws_next("n", ssd_blk(g, 2 + j))
                bank = rot.next()
                inproj(view, wb, bank, hi)
                conv_block(bank, 2 * g + j, xTg[:, par, j, :], [B_xTg[par][j]])

            def a_b():
                view, wb = ws_next("n", ssd_blk(g, 4))
                bank = rot.next()
                inproj(view, wb, bank, hi)
                conv_block(bank, 16 + g, BTg[:, par, :], [B_BTg[par]])

            def a_c():
                view, wb = ws_next("n", ssd_blk(g, 5))
                bank = rot.next()
                inproj(view, wb, bank, hi)
                conv_block(bank, 24 + g, CTg[:, par, :], [B_CTg[par]], only_halo=not main_)

            units = []
            if main_:
                units += [lambda j=j: a_z(j) for j in range(2)]
            units += [lambda j=j: a_x(j) for j in range(2)]
            units.append(a_b)
            if main_ or last_pre_:
                units.append(a_c)
            return units

        def ht_units(psn):
            hb = hbufs[hsel(psn)]
            Bh = B_hTs[hsel(psn)]
            units = []

            pend = []

            def sq_part(k0):
                mm_part()
                for kc in range(k0, k0 + 2):
                    xi = load_x(psn, kc)
                    si = sqp_rot.next()
                    fw.issue("act", lambda e: e.activation(out=sqp[:, si, :], in_=xt[:, xi, :], func=AF.Square),
                             reads=[B_xt[xi]], writes=[B_sqp[si]])
                    pend.append((kc, si))

            def mm_part():
                while pend:
                    kc, si = pend.pop(0)
                    fw.issue("pe", lambda e: e.matmul(pb[4][:], lhsT=ones[:], rhs=sqp[:, si, :], start=(kc == 0), stop=(kc == 15)),
                             reads=[B_sqp[si], B_const], writes=[B_pb[4]])
                    if kc == 15:
                        rstd_from_psum(4, 1.0 / D, 10)

            def sc_part(k0):
                for kc in range(k0, k0 + 4):
                    xi = load_x(psn, kc)
                    fw.issue("dve", lambda e: e.scalar_tensor_tensor(out=hb[:, kc, :], in0=xt[:, xi, :],
                                                                     scalar=cols[:, C_NW + kc:C_NW + kc + 1],
                                                                     in1=t32[:, 10, :], op0=ALU.mult, op1=ALU.mult),
                             reads=[B_xt[xi], B_cols, B_t32[10]], writes=[Bh[kc]])

            units += [lambda k0=k0: sq_part(k0) for k0 in range(0, 16, 2)]
            units.append(mm_part)
            units += [lambda k0=k0: sc_part(k0) for k0 in (0, 4, 8, 12)]
            return units

        def dt_phase(psn):
            hc = hbufs[hsel(psn)]
            Bh = list(dict.fromkeys(B_hTs[hsel(psn)]))
            for c in range(4):
                fw.group("pe", [lambda e, kc=kc: e.matmul(pb[6][:, c * 32:(c + 1) * 32], lhsT=hc[:, kc, c * 128:(c + 1) * 128],
                                                          rhs=wdt[:, kc, :], start=(kc == 0), stop=(kc == 15))
                                for kc in range(16)], reads=Bh + [B_wdt], writes=[B_pb[6]])
            p6 = pb[6][:, 0:128].rearrange("p (c h) -> p c h", c=4)
            dtb_bc = rows[:, 0:32].unsqueeze(1).broadcast_to([128, 4, 32])
            negA_bc = negA[:].unsqueeze(1).broadcast_to([128, 4, 32])
            fw.issue("dve", lambda e: e.tensor_tensor(out=dtmp[:], in0=p6, in1=dtb_bc, op=ALU.add),
                     reads=[B_pb[6], B_rows], writes=[B_dtw])
            fw.issue("act", lambda e: e.activation(out=dtmp[:], in_=dtmp[:], func=AF.Exp), reads=[B_dtw], writes=[B_dtw])
            fw.issue("act", lambda e: e.activation(out=dtt[:], in_=dtmp[:], func=AF.Ln, bias=1.0, scale=1.0),
                     reads=[B_dtw], writes=[B_dt])
            fw.issue("dve", lambda e: e.tensor_tensor(out=adt[:], in0=dtt[:], in1=negA_bc, op=ALU.mult),
                     reads=[B_dt, B_const], writes=[B_adt])
            fw.issue("dve", lambda e: e.tensor_copy(out=adt_hi[:], in_=adt[:]), reads=[B_adt], writes=[B_adt])
            fw.issue("dve", lambda e: e.tensor_tensor(out=adt_lo[:], in0=adt[:], in1=adt_hi[:], op=ALU.subtract),
                     reads=[B_adt], writes=[B_adt])
            mms = []
            for c in range(4):
                mms.append(lambda e, c=c: e.matmul(pb[6][:, 128 + c * 32:128 + (c + 1) * 32], lhsT=tri[:], rhs=adt_hi[:, c, :], start=True, stop=False))
                mms.append(lambda e, c=c: e.matmul(pb[6][:, 128 + c * 32:128 + (c + 1) * 32], lhsT=tri[:], rhs=adt_lo[:, c, :], start=False, stop=True))
                mms.append(lambda e, c=c: e.matmul(pb[6][:, 256 + c * 32:256 + (c + 1) * 32], lhsT=ones[:], rhs=adt_hi[:, c, :], start=True, stop=False))
                mms.append(lambda e, c=c: e.matmul(pb[6][:, 256 + c * 32:256 + (c + 1) * 32], lhsT=ones[:], rhs=adt_lo[:, c, :], start=False, stop=True))
            fw.group("pe", mms, reads=[B_adt, B_const], writes=[B_pb[6]])
            fw.issue("act", lambda e: e.copy(out=acum[:].rearrange("p c h -> p (c h)"), in_=pb[6][:, 128:256]),
                     reads=[B_pb[6]], writes=[B_dt])
            fw.issue("act", lambda e: e.copy(out=aend[:].rearrange("p c h -> p (c h)"), in_=pb[6][:, 256:384]),
                     reads=[B_pb[6]], writes=[B_dt])
            fw.issue("dve", lambda e: e.tensor_tensor(out=dtmp[:], in0=aend[:], in1=acum[:], op=ALU.subtract),
                     reads=[B_dt], writes=[B_dtw])
            fw.issue("act", lambda e: e.activation(out=dtmp[:], in_=dtmp[:], func=AF.Exp), reads=[B_dtw], writes=[B_dtw])
            fw.issue("dve", lambda e: e.tensor_tensor(out=w2[:], in0=dtmp[:], in1=dtt[:], op=ALU.mult),
                     reads=[B_dtw, B_dt], writes=[B_dt])
            fw.issue("act", lambda e: e.activation(out=edec[:], in_=aend[:], func=AF.Exp), reads=[B_dt], writes=[B_dt])

        xpre_rot = Rot([0, 1, 2])

        def conv_block(bank, ci, dst_ap, dst_bufs, only_halo=False):
            xi = xpre_rot.next()
            fw.issue("dve", lambda e: e.tensor_copy(out=xpre[:, xi, 0:3], in_=halo[:, ci, :]),
                     reads=[B_halo[ci]], writes=[B_xpre[xi]])
            fw.issue("act", lambda e: e.copy(out=xpre[:, xi, 3:515], in_=pb[bank][:]),
                     reads=[B_pb[bank]], writes=[B_xpre[xi]])
            fw.issue("dve", lambda e: e.tensor_copy(out=halo[:, ci, :], in_=xpre[:, xi, 512:515]),
                     reads=[B_xpre[xi]], writes=[B_halo[ci]])
            if only_halo:
                return
            ct = 12 + (ci % 2)
            fw.issue("act", lambda e: e.activation(out=t32[:, ct, :], in_=xpre[:, xi, 0:512], func=AF.Identity,
                                                   bias=cols[:, C_CB + ci:C_CB + ci + 1],
                                                   scale=cols[:, C_CW + ci:C_CW + ci + 1]),
                     reads=[B_xpre[xi], B_cols], writes=[B_t32[ct]])
            for k in range(1, 4):
                fw.issue("dve", lambda e, k=k: e.scalar_tensor_tensor(out=t32[:, ct, :], in0=xpre[:, xi, k:k + 512],
                                                                      scalar=cols[:, C_CW + 32 * k + ci:C_CW + 32 * k + ci + 1],
                                                                      in1=t32[:, ct, :], op0=ALU.mult, op1=ALU.add),
                         reads=[B_xpre[xi], B_t32[ct]], writes=[B_t32[ct]])
            fw.issue("act", lambda e: e.activation(out=dst_ap, in_=t32[:, ct, :], func=AF.Silu),
                     reads=[B_t32[ct]], writes=dst_bufs)

        def bc3(ap2, n):
            return ap2.unsqueeze(2).broadcast_to([128, ap2.shape[1], n])

        def do_pass(ps):
            main = ps >= 4
            last_pre = ps == 3
            if ps == 4:
                flag = cols[:, C_FLAG:C_FLAG + 1]
                fw.issue("dve", lambda e: e.tensor_scalar(out=H[:].rearrange("p g n -> p (g n)"), in0=H[:].rearrange("p g n -> p (g n)"),
                                                          scalar1=flag, scalar2=None, op0=ALU.mult),
                         reads=[B_cols], writes=B_H)
                fw.issue("dve", lambda e: e.tensor_scalar(out=halo[:].rearrange("p g n -> p (g n)"), in0=halo[:].rearrange("p g n -> p (g n)"),
                                                          scalar1=flag, scalar2=None, op0=ALU.mult),
                         reads=[B_cols], writes=B_halo)

            cur["h"] = hbufs[hsel(ps)]
            cur["B_h"] = B_hTs[hsel(ps)]
            hcur = cur["h"]
            Bhcur = list(dict.fromkeys(cur["B_h"]))
            dt_early = ps >= 5
            _cp(ps * 10 + 3)
            p3b = p3t[:]

            def ssd_steps(g):
                par = g % 2
                if main:
                    mms = []
                    for c in range(4):
                        mms.append(lambda e, c=c: e.matmul(pb[7][0:4, c * 128:(c + 1) * 128], lhsT=adt_hi[:, c, 4 * g:4 * g + 4], rhs=tri[:], start=True, stop=False))
                        mms.append(lambda e, c=c: e.matmul(pb[7][0:4, c * 128:(c + 1) * 128], lhsT=adt_lo[:, c, 4 * g:4 * g + 4], rhs=tri[:], start=False, stop=True))
                    fw.group("pe", mms, reads=[B_adt, B_const], writes=[B_pb[7]])
                    fw.issue("act", lambda e: e.copy(out=t32[0:4, 10, :], in_=pb[7][0:4, :]), reads=[B_pb[7]], writes=[B_t32[10]])
                    fw.issue("dve", lambda e: e.tensor_copy(out=acT[0:4, 0, :], in_=t32[0:4, 10, :]), reads=[B_t32[10]], writes=[B_acT])
                    fw.issue("dve", lambda e: e.tensor_tensor(out=acT[0:4, 1, :], in0=t32[0:4, 10, :], in1=acT[0:4, 0, :], op=ALU.subtract),
                             reads=[B_t32[10], B_acT], writes=[B_acT])
                for c in range(4):
                    cs = slice(c * 128, (c + 1) * 128)
                    trs = [lambda e, j=j: e.transpose(p3b[:, j * 128:(j + 1) * 128], xTg[:, par, j, cs], ident[:]) for j in range(2)]
                    trs.append(lambda e: e.transpose(p3b[:, 256:384], BTg[:, par, cs], ident[:]))
                    fw.group("pe", trs, reads=[B_xTg[par][0], B_xTg[par][1], B_BTg[par], B_const], writes=[B_pb[3]])
                    fw.issue("dve", lambda e: e.tensor_copy(out=btok[:, c, :], in_=p3b[:, 256:384]), reads=[B_pb[3]], writes=[B_btok[c]])
                    xtk = p3b[:, 0:256].rearrange("p (h q) -> p h q", h=4)
                    fw.issue("dve", lambda e: e.tensor_tensor(out=xdec[:, c, :].rearrange("p (h q) -> p h q", h=4), in0=xtk,
                                                              in1=bc3(w2[:, c, 4 * g:4 * g + 4], 64), op=ALU.mult),
                             reads=[B_pb[3], B_dt], writes=[B_xdec[c]])
                    if main:
                        for q in range(2):
                            fw.issue("dve", lambda e: e.tensor_tensor(out=xdtpad[:, c, q::2, q * 64:(q + 1) * 64], in0=xtk[:, q::2, :],
                                                                      in1=bc3(dtt[:, c, 4 * g + q:4 * g + 4:2], 64), op=ALU.mult),
                                     reads=[B_pb[3], B_dt], writes=[B_xdt[c]])
                    yield
                Hg = H[:, g, :].rearrange("p (h q) -> p h q", h=4)
                if main:
                    for q in range(2):
                        fw.issue("act", lambda e: e.copy(out=Hpad[:, par, 0, q::2, q * 64:(q + 1) * 64], in_=Hg[:, q::2, :]),
                                 reads=[B_H[g]], writes=[B_Hpad[par][0]])
                for c in range(4):
                    hs = slice((c % 2) * 256, (c % 2) * 256 + 256)
                    fw.issue("pe", lambda e: e.matmul(pb[7][:, hs], lhsT=btok[:, c, :], rhs=xdec[:, c, :], start=True, stop=True),
                             reads=[B_btok[c], B_xdec[c]], writes=[B_pb[7]])
                    fw.issue("dve", lambda e: e.tensor_tensor(out=Hg, in0=Hg, in1=bc3(edec[:, c, 4 * g:4 * g + 4], 64), op=ALU.mult),
                             reads=[B_dt], writes=[B_H[g]])
                    fw.issue("dve", lambda e: e.tensor_tensor(out=H[:, g, :], in0=H[:, g, :], in1=pb[7][:, hs], op=ALU.add),
                             reads=[B_pb[7]], writes=[B_H[g]])
                    if main and c < 3:
                        for q in range(2):
                            fw.issue("act", lambda e: e.copy(out=Hpad[:, par, c + 1, q::2, q * 64:(q + 1) * 64], in_=Hg[:, q::2, :]),
                                     reads=[B_H[g]], writes=[B_Hpad[par][c + 1]])
                    if c % 2 == 1:
                        yield
                if main:
                    fw.group("pe", [lambda e, c=c: e.matmul(pb[4][:, c * 128:(c + 1) * 128], lhsT=BTg[:, par, c * 128:(c + 1) * 128],
                                                            rhs=CTg[:, par, c * 128:(c + 1) * 128], start=True, stop=True) for c in range(4)],
                             reads=[B_BTg[par], B_CTg[par]], writes=[B_pb[4]])
                    fw.issue("act", lambda e: e.copy(out=t32[:, 9, :], in_=pb[4][:]), reads=[B_pb[4]], writes=[B_t32[9]])
                    for hl in range(4):
                        h = 4 * g + hl
                        rb = rot.next()
                        fw.group("pe", [lambda e: e.matmul(pb[rb][:], lhsT=sel[:, hl, :], rhs=acT[:, 0, :], start=True, stop=False),
                                        lambda e: e.matmul(pb[rb][:], lhsT=sel[:, hl, :], rhs=acT[:, 1, :], start=False, stop=True)],
                                 reads=[B_acT, B_const], writes=[B_pb[rb]])
                        ts = 0 + (hl % 2)
                        te = 2 + (hl % 2)
                        for c in range(4):
                            fw.issue("dve", lambda e: e.scalar_tensor_tensor(
                                out=t32[:, ts, c * 128:(c + 1) * 128], in0=pb[rb][:, c * 128:(c + 1) * 128],
                                scalar=acum[:, c, h:h + 1], in1=maskneg[:], op0=ALU.subtract, op1=ALU.add),
                                reads=[B_pb[rb], B_dt, B_const], writes=[B_t32[ts]])
                        fw.issue("act", lambda e: e.activation(out=t32[:, te, :], in_=pb[rb][:], func=AF.Exp),
                                 reads=[B_pb[rb]], writes=[B_t32[te]])
                        fw.issue("act", lambda e: e.activation(out=t32[:, ts, :], in_=t32[:, ts, :], func=AF.Exp),
                                 reads=[B_t32[ts]], writes=[B_t32[ts]])
                        fw.issue("dve", lambda e: e.tensor_tensor(out=CE[:, hl, :], in0=t32[:, te, :], in1=CTg[:, par, :], op=ALU.mult),
                                 reads=[B_t32[te], B_CTg[par]], writes=[B_CE[hl]])
                        fw.issue("dve", lambda e: e.tensor_tensor(out=scT[:, hl, :], in0=t32[:, ts, :], in1=t32[:, 9, :], op=ALU.mult),
                                 reads=[B_t32[ts], B_t32[9]], writes=[B_scT[hl]])
                        yield

                if main:
                    for c in range(4):
                        cs = slice(c * 128, (c + 1) * 128)
                        for j in range(2):
                            mms = []
                            for hl in (2 * j, 2 * j + 1):
                                mms.append(lambda e, hl=hl: e.matmul(pb[5 + j][:, cs], lhsT=xdtpad[:, c, hl, :], rhs=scT[:, hl, cs],
                                                                     start=(hl == 2 * j), stop=False))
                            for hl in (2 * j, 2 * j + 1):
                                mms.append(lambda e, hl=hl: e.matmul(pb[5 + j][:, cs], lhsT=Hpad[:, par, c, hl, :], rhs=CE[:, hl, cs],
                                                                     start=False, stop=(hl == 2 * j + 1)))
                            fw.group("pe", mms, reads=[B_xdt[c], B_scT[2 * j], B_scT[2 * j + 1], B_Hpad[par][c], B_CE[2 * j], B_CE[2 * j + 1]],
                                     writes=[B_pb[5 + j]])
                        if c % 2 == 1:
                            yield

                if main:
                    for j in range(2):
                        blk = 2 * g + j
                        ty = 4 + j
                        fw.issue("dve", lambda e: e.scalar_tensor_tensor(out=t32[:, ty, :], in0=xTg[:, par, j, :],
                                                                         scalar=cols[:, C_DS + blk:C_DS + blk + 1],
                                                                         in1=pb[5 + j][:], op0=ALU.mult, op1=ALU.add),
                                 reads=[B_xTg[par][j], B_cols, B_pb[5 + j]], writes=[B_t32[ty]])
                        fw.issue("dve", lambda e: e.tensor_tensor(out=t32[:, ty, :], in0=t32[:, ty, :], in1=szb[:, par, j, :], op=ALU.mult),
                                 reads=[B_t32[ty], B_szb[par][j]], writes=[B_t32[ty]])
                        si = sq_rot.next()
                        fw.issue("act", lambda e: e.activation(out=sq[:, si, :], in_=t32[:, ty, :], func=AF.Square),
                                 reads=[B_t32[ty]], writes=[B_sq[si]])
                        fw.issue("pe", lambda e: e.matmul(pb[7][:], lhsT=ones[:], rhs=sq[:, si, :], start=(j == 0), stop=(j == 1)),
                                 reads=[B_sq[si], B_const], writes=[B_pb[7]])
                    rstd_from_psum(7, 1.0 / 256.0, 6)
                    for j in range(2):
                        blk = 2 * g + j
                        ty = 4 + j
                        fw.issue("dve", lambda e: e.scalar_tensor_tensor(out=yT[:, 16 + blk, :], in0=t32[:, ty, :],
                                                                         scalar=cols[:, C_SNW + blk:C_SNW + blk + 1],
                                                                         in1=t32[:, 6, :], op0=ALU.mult, op1=ALU.mult),
                                 reads=[B_t32[ty], B_cols, B_t32[6]], writes=[B_y[16 + blk]])
                    yield

            def v_unit(j):
                view, wb = ws_next("w", j)
                for c in range(4):
                    bank = rot.next()
                    fw.group("pe", [lambda e, kc=kc: e.matmul(pb[bank][:], lhsT=hcur[:, kc, c * 128:(c + 1) * 128],
                                                              rhs=view[:, kc, :], start=(kc == 0), stop=(kc == 15))
                                    for kc in range(16)], reads=list(wb) + Bhcur, writes=[B_pb[bank]])
                    fw.issue("act", lambda e: e.copy(out=vn[:, c, j * 512:(j + 1) * 512], in_=pb[bank][:]),
                             reads=[B_pb[bank]], writes=[B_vm[c]])
                    fw.issue("dve", lambda e: e.tensor_reduce(out=st1[:, c, j:j + 1], in_=pb[bank][:], axis=AX.X, op=ALU.add),
                             reads=[B_pb[bank]], writes=[B_st])
                    tq = 7 + (c % 2)
                    fw.issue("act", lambda e: e.activation(out=t32[:, tq, :], in_=pb[bank][:], func=AF.Square),
                             reads=[B_pb[bank]], writes=[B_t32[tq]])
                    fw.issue("dve", lambda e: e.tensor_reduce(out=st2[:, c, j:j + 1], in_=t32[:, tq, :], axis=AX.X, op=ALU.add),
                             reads=[B_t32[tq]], writes=[B_st])

            def norm_unit():
                fw.issue("dve", lambda e: e.tensor_reduce(out=stm[:, 0, :], in_=st1[:], axis=AX.X, op=ALU.add), reads=[B_st], writes=[B_st])
                fw.issue("dve", lambda e: e.tensor_reduce(out=stm[:, 1, :], in_=st2[:], axis=AX.X, op=ALU.add), reads=[B_st], writes=[B_st])
                fw.issue("dve", lambda e: e.tensor_scalar(out=stm[:, 2, :], in0=stm[:, 0, :], scalar1=1.0 / E, scalar2=None, op0=ALU.mult),
                         reads=[B_st], writes=[B_st])
                fw.issue("dve", lambda e: e.tensor_tensor(out=stm[:, 3, :], in0=stm[:, 2, :], in1=stm[:, 2, :], op=ALU.mult),
                         reads=[B_st], writes=[B_st])
                fw.issue("dve", lambda e: e.scalar_tensor_tensor(out=stm[:, 4, :], in0=stm[:, 1, :], scalar=1.0 / E, in1=stm[:, 3, :],
                                                                 op0=ALU.mult, op1=ALU.subtract), reads=[B_st], writes=[B_st])
                fw.issue("act", lambda e: e.activation(out=stm[:, 5, :], in_=stm[:, 4, :], func=AF.Sqrt, bias=EPS, scale=1.0),
                         reads=[B_st], writes=[B_st])
                fw.issue("dve", lambda e: e.reciprocal(out=stm[:, 5, :], in_=stm[:, 5, :]), reads=[B_st], writes=[B_st])
                fw.issue("dve", lambda e: e.scalar_tensor_tensor(out=stm[:, 6, :], in0=stm[:, 2, :], scalar=-1.0, in1=stm[:, 5, :],
                                                                 op0=ALU.mult, op1=ALU.mult), reads=[B_st], writes=[B_st])
                for c in range(4):
                    fw.issue("dve", lambda e: e.tensor_scalar(out=vn[:, c, :], in0=vn[:, c, :], scalar1=stm[:, 5, c:c + 1],
                                                              scalar2=stm[:, 6, c:c + 1], op0=ALU.mult, op1=ALU.add),
                             reads=[B_st], writes=[B_vm[c]])

            def cb_unit(cb):
                g = cb // 2
                j = cb % 2
                view, wb = ws_next("n", gm_blk(g, 2 + j))
                zb = rot.next()
                inproj(view, wb, zb)
                fw.issue("act", lambda e: e.activation(out=t32[:, 11, :], in_=pb[zb][:], func=AF.Silu),
                         reads=[B_pb[zb]], writes=[B_t32[11]])
                view, wb = ws_next("n", gm_blk(g, j))
                ub = rot.next()
                inproj(view, wb, ub)
                fw.issue("dve", lambda e: e.tensor_tensor(out=t32[:, 11, :], in0=t32[:, 11, :], in1=pb[ub][:], op=ALU.mult),
                         reads=[B_t32[11], B_pb[ub]], writes=[B_t32[11]])
                mb = rot.next()
                fw.group("pe", [lambda e, c=c: e.matmul(pb[mb][:, c * 128:(c + 1) * 128], lhsT=vn[:, c, cb * 128:(cb + 1) * 128],
                                                        rhs=WgT[:, g, :], start=True, stop=True) for c in range(4)],
                         reads=B_vm + [B_const], writes=[B_pb[mb]])
                fw.issue("dve", lambda e: e.scalar_tensor_tensor(
                    out=t32[:, 14, :].rearrange("p (c t) -> p c t", c=4), in0=pb[mb][:].rearrange("p (c t) -> p c t", c=4),
                    scalar=cols[:, C_LNW + cb:C_LNW + cb + 1], in1=Kc[:, cb, :].unsqueeze(1).broadcast_to([128, 4, 128]),
                    op0=ALU.mult, op1=ALU.add), reads=[B_pb[mb], B_cols, B_const], writes=[B_t32[14]])
                fw.issue("dve", lambda e: e.tensor_tensor(out=yT[:, cb, :], in0=t32[:, 11, :], in1=t32[:, 14, :], op=ALU.mult),
                         reads=[B_t32[11], B_t32[14]], writes=[B_y[cb]])

            if ps not in a0_done:
                for u in A_units(ps, 0):
                    u()
            if not dt_early:
                dt_phase(ps)
            gm = []
            if not main:
                gm += ht_units(ps + 1)
            if main:
                gm += [lambda j=j: v_unit(j) for j in range(4)]
                gm.append(norm_unit)
                gm += [lambda cb=cb: cb_unit(cb) for cb in range(16)]
            gquota = [2, 3, 3, 3, 3, 3, 3, 3] if main else [2, 2, 2, 2, 2, 2, 2, 2]
            for g in range(8):
                if g == 1:
                    _cp(ps * 10 + 4)
                if g < 7:
                    nextA = A_units(ps, g + 1)
                elif ps < 3:
                    nextA = A_units(ps + 1, 0)
                    a0_done.add(ps + 1)
                else:
                    nextA = []
                gq = gquota[g]
                k = 0
                for _ in ssd_steps(g):
                    if k % 2 == 0 and nextA:
                        nextA.pop(0)()
                    elif gm and gq > 0:
                        gm.pop(0)()
                        gq -= 1
                    elif nextA:
                        nextA.pop(0)()
                    k += 1
                while nextA:
                    nextA.pop(0)()
            while gm:
                gm.pop(0)()

            _cp(ps * 10 + 5)
            if not main:
                return

            _cp(ps * 10 + 7)
            grot = Rot([0, 1, 2])
            brot = Rot([4, 5, 6, 7])
            for d in range(16):
                gt = []
                for n in range(2):
                    view, wb = ws_next("n", br_blk(d, n))
                    bank = grot.next()
                    inproj(view, wb, bank)
                    tg = 4 + 2 * (d % 2) + n
                    fw.issue("act", lambda e, bank=bank, tg=tg, n=n, d=d: e.activation(out=t32[:, tg, :], in_=pb[bank][:], func=AF.Sigmoid,
                                                                                       bias=cols[:, C_BG + 16 * n + d:C_BG + 16 * n + d + 1], scale=1.0),
                             reads=[B_pb[bank], B_cols], writes=[B_t32[tg]])
                    gt.append(tg)
                bb = []
                for n in range(2):
                    view, wb = ws_next("n", br_blk(d, 2 + n))
                    bank = brot.next()
                    fw.group("pe", [lambda e, ec=ec, n=n, bank=bank, view=view: e.matmul(pb[bank][:], lhsT=view[:, ec, :], rhs=yT[:, 16 * n + ec, :],
                                                                                         start=(ec == 0), stop=(ec == 15)) for ec in range(16)],
                             reads=list(wb) + B_y[16 * n:16 * n + 16], writes=[B_pb[bank]])
                    bb.append(bank)
                for n in range(2):
                    fw.issue("dve", lambda e, n=n, tg=gt[n], bank=bb[n]: e.tensor_tensor(out=t32[:, tg, :], in0=t32[:, tg, :], in1=pb[bank][:], op=ALU.mult),
                             reads=[B_t32[gt[n]], B_pb[bb[n]]], writes=[B_t32[gt[n]]])
                fw.issue("dve", lambda e, d=d, g0=gt[0], g1=gt[1]: e.tensor_tensor(out=mg[:, d, :], in0=t32[:, g0, :], in1=t32[:, g1, :], op=ALU.add),
                         reads=[B_t32[gt[0]], B_t32[gt[1]]], writes=[B_vm[d // 4]])

            _cp(ps * 10 + 8)
            orot = Rot([0, 1, 2])
            pro = []
            if ps < 7:
                pro = ht_units(ps + 1) + [lambda: dt_phase(ps + 1)] + A_units(ps + 1, 0)
                a0_done.add(ps + 1)
            for d in range(16):
                if pro and d >= 1:
                    pro.pop(0)()
                if pro and d >= 6:
                    pro.pop(0)()
                view, wb = ws_next("n", out_blk(d))
                bank = orot.next()
                fw.group("pe", [lambda e, dc=dc, bank=bank, view=view: e.matmul(pb[bank][:], lhsT=view[:, dc, :], rhs=mg[:, dc, :],
                                                                               start=(dc == 0), stop=(dc == 15)) for dc in range(16)],
                         reads=list(wb) + B_vm, writes=[B_pb[bank]])
                xi = load_x(ps, d)
                fw.issue("dve", lambda e, d=d, xi=xi, bank=bank: e.tensor_tensor(out=xnew[:, d, :], in0=xt[:, xi, :], in1=pb[bank][:], op=ALU.add),
                         reads=[B_xt[xi], B_pb[bank]], writes=[B_y[2 * d], B_y[2 * d + 1]])
                si = sq_rot.next()
                fw.issue("act", lambda e, d=d, si=si: e.activation(out=sq[:, si, :], in_=xnew[:, d, :], func=AF.Square),
                         reads=[B_y[2 * d], B_y[2 * d + 1]], writes=[B_sq[si]])
                fw.issue("pe", lambda e, d=d, si=si: e.matmul(pb[7][:], lhsT=ones[:], rhs=sq[:, si, :], start=(d == 0), stop=(d == 15)),
                         reads=[B_sq[si], B_const], writes=[B_pb[7]])
            while pro:
                pro.pop(0)()
            rstd_from_psum(7, 1.0 / D, 11)
            for d in range(16):
                to = 7 + (d % 3)
                fw.issue("dve", lambda e, d=d, to=to: e.scalar_tensor_tensor(out=t32[:, to, :], in0=xnew[:, d, :],
                                                                             scalar=cols[:, C_FW + d:C_FW + d + 1], in1=t32[:, 11, :],
                                                                             op0=ALU.mult, op1=ALU.mult),
                         reads=[B_y[2 * d], B_y[2 * d + 1], B_cols, B_t32[11]], writes=[B_t32[to]])
                fw.issue("sp", lambda e, d=d, to=to: e.dma_start(out=out_d[ps - 4][:, d * 512:(d + 1) * 512], in_=t32[:, to, :]),
                         reads=[B_t32[to]], dma_key=k_out[to])

        try:
            _cp(1)
            for u in ht_units(0):
                u()
            for ps in range(8):
                do_pass(ps)
                _cp(ps * 10 + 9)
        except _Stop:
            pass
        for to in (7, 8, 9):
            fw.final_wait("sp", k_out[to])
        if dry:
            return order
        assert DEBUG_STOP is not None or wstate["next_use"] == len(plan)
        fw.emit()
    return nc


def _blk(W, c0, n=128):
    w = W[:, c0:c0 + n].reshape(16, 128, n).transpose(1, 0, 2)
    return np.ascontiguousarray(w).reshape(128, 16 * n)


def _col(v):
    v = np.asarray(v, np.float32)
    return v.reshape(-1, 128).T


def prep_shared(inp):
    w_in = np.asarray(inp["w_in"][0], np.float32)
    w_br = np.asarray(inp["w_branch"][0], np.float32)
    w_out = np.asarray(inp["w_out"][0], np.float32)
    wn = np.empty((NBLK, 128, 2048), np.float32)
    for g in range(8):
        wn[ssd_blk(g, 0)] = _blk(w_in, O3 + (2 * g) * 128)
        wn[ssd_blk(g, 1)] = _blk(w_in, O3 + (2 * g + 1) * 128)
        wn[ssd_blk(g, 2)] = _blk(w_in, O4 + (2 * g) * 128)
        wn[ssd_blk(g, 3)] = _blk(w_in, O4 + (2 * g + 1) * 128)
        wn[ssd_blk(g, 4)] = _blk(w_in, OB + g * 128)
        wn[ssd_blk(g, 5)] = _blk(w_in, OC + g * 128)
        wn[gm_blk(g, 0)] = _blk(w_in, 0 + (2 * g) * 128)
        wn[gm_blk(g, 1)] = _blk(w_in, 0 + (2 * g + 1) * 128)
        wn[gm_blk(g, 2)] = _blk(w_in, O2 + (2 * g) * 128)
        wn[gm_blk(g, 3)] = _blk(w_in, O2 + (2 * g + 1) * 128)
    for d in range(16):
        wn[br_blk(d, 0)] = _blk(w_in, O6 + d * 128)
        wn[br_blk(d, 1)] = _blk(w_in, O6 + 2048 + d * 128)
        wn[br_blk(d, 2)] = _blk(w_br[0], d * 128)
        wn[br_blk(d, 3)] = _blk(w_br[1], d * 128)
        wn[out_blk(d)] = _blk(w_out, d * 128)
    wv = np.empty((4, 128, 8192), np.float32)
    for j in range(4):
        wv[j] = _blk(w_in, O1 + j * 512, 512)
    wdt = _blk(w_in, O5, 32)
    cols = np.zeros((128, NCOL), np.float32)
    cols[:, C_NW:C_NW + 16] = _col(inp["norm_w"][0])
    cols[:, C_LNW:C_LNW + 16] = _col(inp["ln_v_w"][0])
    cols[:, C_LNB:C_LNB + 16] = _col(inp["ln_v_b"][0])
    cw = np.asarray(inp["conv_w"][0], np.float32)
    for k in range(4):
        cols[:, C_CW + 32 * k:C_CW + 32 * k + 32] = _col(cw[k])
    cols[:, C_CB:C_CB + 32] = _col(inp["conv_b"][0])
    cols[:, C_SNW:C_SNW + 16] = _col(inp["ssm_norm_w"][0])
    bg = np.asarray(inp["b_gate"][0], np.float32)
    cols[:, C_BG:C_BG + 16] = _col(bg[0])
    cols[:, C_BG + 16:C_BG + 32] = _col(bg[1])
    cols[:, C_FW:C_FW + 16] = _col(inp["final_norm_w"])
    cols[:, C_DS:C_DS + 16] = _col(np.repeat(np.asarray(inp["d_skip"][0], np.float32), 64))
    rows = np.zeros((128, 64), np.float32)
    rows[:, 0:32] = np.asarray(inp["dt_bias"][0], np.float32)[None, :]
    rows[:, 32:64] = np.asarray(inp["a_log"][0], np.float32)[None, :]
    bsb = np.ascontiguousarray(np.broadcast_to(np.asarray(inp["b_spatial"][0], np.float32).reshape(1, 1024), (128, 1024)))
    ws = np.asarray(inp["w_spatial"][0], np.float32)
    wst = np.ascontiguousarray(ws.transpose(2, 0, 1)).reshape(128, 1024)
    return dict(wn=wn, wv=wv, wdt=np.ascontiguousarray(wdt), cols=cols, rows=rows, bsb=bsb, wst=wst)


def _xtiles(xb, t0):
    seg = xb[t0:t0 + 2048]
    a = seg.reshape(4, 512, 16, 128).transpose(0, 3, 2, 1)
    return np.ascontiguousarray(a).reshape(4, 128, 8192)


_NC_CACHE = {}


def kernel(**inputs):
    x = np.asarray(inputs["x"], np.float32)
    shared = prep_shared(inputs)
    if "nc" not in _NC_CACHE:
        _NC_CACHE["nc"] = build_program()
    nc = _NC_CACHE["nc"]
    in_maps = []
    for c in range(8):
        b, half = c // 2, c % 2
        xa = np.concatenate([_xtiles(x[b], 0), _xtiles(x[b], half * 2048)], axis=0)
        m = dict(shared)
        cols = shared["cols"].copy()
        cols[:, C_FLAG] = float(half)
        m["cols"] = cols
        m["x_all"] = xa
        in_maps.append(m)
    res = run_bass_kernel_spmd(nc, in_maps, core_ids=list(range(8)))
    out = np.empty((4, 4096, 2048), np.float32)
    for c in range(8):
        b, half = c // 2, c % 2
        o = np.asarray(res.results[c]["out"]).reshape(4, 128, 16, 512)
        o = o.transpose(0, 3, 2, 1).reshape(2048, 2048)
        out[b, half * 2048:(half + 1) * 2048] = o
    return out
```

```python
import numpy as np
from contextlib import ExitStack
import concourse.bass as bass
import concourse.mybir as mybir
from concourse.bass_utils import run_bass_kernel_spmd

F32 = mybir.dt.float32
BF16 = mybir.dt.bfloat16
AF = mybir.ActivationFunctionType
ALU = mybir.AluOpType
AX = mybir.AxisListType

D = 2048
E = 2048
O1, O2, O3, O4 = 2048, 4096, 6144, 8192
OB = O4 + 2048
OC = OB + 1024
O5 = O4 + 4096
O6 = O5 + 32
EPS = 1e-5
NSLOT = 8
T = 512
NEG = -30000.0

C_NW, C_LNW, C_LNB, C_CW, C_CB, C_SNW, C_BG, C_FW, C_DS, C_FLAG = 0, 16, 32, 48, 176, 208, 224, 256, 272, 288
NCOL = 289


class Tok:
    __slots__ = ("key", "count")

    def __init__(self, key, count):
        self.key = key
        self.count = count


class Buf:
    __slots__ = ("w", "r", "excl")

    def __init__(self, excl=False):
        self.w = []
        self.r = []
        self.excl = excl


class _Rec:
    def __init__(self):
        self.call = None

    def __getattr__(self, name):
        def f(*a, **k):
            self.call = (name, a, k)
            return self
        return f


def _free(ap):
    n = 1
    for d in ap.shape[1:]:
        n *= int(d)
    return n


def _record(fn):
    r = _Rec()
    fn(r)
    name, a, k = r.call
    dur = 0.3
    tbl = None
    if name == "activation":
        tbl = _TBL.get(str(k.get("func")), None)
    try:
        if name == "matmul":
            rhs = k.get("rhs", a[2] if len(a) > 2 else None)
            dur = 0.03 + _free(rhs) * 0.00043
        elif name == "transpose":
            dur = 0.09
        elif name == "dma_start":
            o = k["out"]
            dur = 2.0 + 128 * _free(o) * 4 / 180e3
        elif name in ("activation", "copy"):
            dur = 0.22 + _free(k["out"]) * 0.00085
        elif name == "memset" or name == "affine_select":
            dur = 0.5
        else:
            dur = 0.12 + _free(k["out"]) * 0.00115
    except Exception:
        pass
    return (lambda e: getattr(e, name)(*a, **k)), dur, tbl


_TBL = {str(AF.Exp): "exp", str(AF.Ln): "ln", str(AF.Silu): "silu", str(AF.Sigmoid): "sigm", str(AF.Sqrt): "sqrt"}


class Node:
    __slots__ = ("id", "eng", "fns", "dur", "deps", "dma_key", "start", "finish", "count", "tbl")

    def __init__(self, nid, eng, fns, dur, deps, dma_key, tbl=None):
        self.tbl = tbl
        self.id = nid
        self.eng = eng
        self.fns = fns
        self.dur = dur
        self.deps = deps
        self.dma_key = dma_key
        self.start = None
        self.finish = None
        self.count = None


class FW:
    ENGS = ("pe", "act", "dve", "pool", "sp")
    WINDOW = 160
    LAT = 0.1

    def __init__(self, nc, es):
        self.nc = nc
        self.es = es
        self.nodes = []
        self.sems = {}
        self.finals = []
        for e in self.ENGS:
            self.newsem(e)

    def newsem(self, key):
        self.sems[key] = self.es.enter_context(self.nc.semaphore("s_" + str(key)))
        return key

    @staticmethod
    def _split(reads, writes):
        writes = list(writes) + [b for b in reads if b.excl]
        reads = [b for b in reads if not b.excl]
        return reads, writes

    def _add(self, eng, fns, dur, reads, writes, dma_key, tbl=None):
        reads, writes = self._split(reads, writes)
        deps = set()
        for b in reads:
            deps.update(b.w)
        for b in writes:
            deps.update(b.w)
            deps.update(b.r)
        nid = len(self.nodes)
        self.nodes.append(Node(nid, eng, fns, dur, deps, dma_key, tbl))
        for b in reads:
            b.r.append(nid)
        for b in writes:
            b.w = [nid]
            b.r = []
        return nid

    def issue(self, eng, fn, reads=(), writes=(), dma_key=None):
        f, dur, tbl = _record(fn)
        return self._add(eng, [f], dur, reads, writes, dma_key, tbl)

    def group(self, eng, fns, reads=(), writes=()):
        rec = [_record(fn) for fn in fns]
        return self._add(eng, [r[0] for r in rec], sum(r[1] for r in rec), reads, writes, None)

    def final_wait(self, eng, key):
        self.finals.append((eng, key))

    def _schedule(self):
        nodes = self.nodes
        queues = {e: [n for n in nodes if n.eng == e] for e in self.ENGS}
        head = {e: 0 for e in self.ENGS}
        free = {e: 0.0 for e in self.ENGS}
        order = {e: [] for e in self.ENGS}
        done = [False] * len(nodes)
        remaining = len(nodes)
        cur_tbl = None
        TLOAD = 1.3
        while remaining:
            best = None
            for e in self.ENGS:
                q = queues[e]
                h = head[e]
                while h < len(q) and done[q[h].id]:
                    h += 1
                head[e] = h
                cnt = 0
                i = h
                while i < len(q) and cnt < self.WINDOW:
                    n = q[i]
                    i += 1
                    if done[n.id]:
                        continue
                    cnt += 1
                    ready = 0.0
                    ok = True
                    for d in n.deps:
                        if not done[d]:
                            ok = False
                            break
                        f = nodes[d].finish + (0.0 if nodes[d].eng == e == "pe" else self.LAT)
                        if f > ready:
                            ready = f
                    if not ok:
                        continue
                    st = max(free[e], ready)
                    pen = TLOAD if (n.tbl is not None and n.tbl != cur_tbl) else 0.0
                    key = (st + pen, n.id)
                    if best is None or key < best[0]:
                        best = (key, e, n)
                    if ready <= free[e] and pen == 0.0:
                        break
            (st, _), e, n = best
            if n.tbl is not None:
                cur_tbl = n.tbl
            if n.dma_key is not None:
                n.finish = st + n.dur
                free[e] = st + 0.15
            else:
                n.finish = st + n.dur
                free[e] = n.finish
            done[n.id] = True
            order[e].append(n)
            remaining -= 1
        return order

    def emit(self):
        order = self._schedule()
        nodes = self.nodes
        cnt = {}
        for e in self.ENGS:
            for n in order[e]:
                key = n.dma_key if n.dma_key is not None else e
                inc = 16 if n.dma_key is not None else 1
                cnt[key] = cnt.get(key, 0) + inc
                n.count = (key, cnt[key])
        ops = {e: [] for e in self.ENGS}
        for e in self.ENGS:
            waited = {}
            for n in order[e]:
                need = {}
                for d in n.deps:
                    dn = nodes[d]
                    key, c = dn.count
                    if e == "pe" and key == "pe":
                        continue
                    if c > need.get(key, 0):
                        need[key] = c
                for key, c in need.items():
                    if waited.get(key, 0) >= c:
                        continue
                    waited[key] = c
                    sem = self.sems[key]
                    ops[e].append(lambda eng, sem=sem, c=c: eng.wait_ge(sem, c))
                for f in n.fns[:-1]:
                    ops[e].append(f)
                key, c = n.count
                sem = self.sems[key]
                inc = 16 if n.dma_key is not None else 1
                last = n.fns[-1]
                ops[e].append(lambda eng, last=last, sem=sem, inc=inc: last(eng).then_inc(sem, inc))
            for (fe, key) in self.finals:
                if fe == e and key in cnt:
                    sem = self.sems[key]
                    c = cnt[key]
                    ops[e].append(lambda eng, sem=sem, c=c: eng.wait_ge(sem, c))
        with self.nc.Block() as block:
            @block.tensor
            def _(e):
                for f in ops["pe"]:
                    f(e)

            @block.scalar
            def _(e):
                for f in ops["act"]:
                    f(e)

            @block.vector
            def _(e):
                for f in ops["dve"]:
                    f(e)

            @block.gpsimd
            def _(e):
                for f in ops["pool"]:
                    f(e)

            @block.sync
            def _(e):
                for f in ops["sp"]:
                    f(e)


def ssd_blk(g, k):
    return g * 6 + k


def gm_blk(g, k):
    return 48 + g * 4 + k


def br_blk(d, k):
    return 80 + d * 4 + k


def out_blk(d):
    return 144 + d


NBLK = 160


def stream_plan(items):
    pos = 0
    plan = []
    for kind, idx in items:
        if kind == "w":
            pos = (pos + 3) // 4 * 4
            plan.append((kind, idx, pos, 4))
            pos += 4
        else:
            plan.append((kind, idx, pos, 1))
            pos += 1
    return plan


def _old_stream_items():
    items = []
    for ps in range(8):
        if ps < 4:
            for g in range(8):
                items.append(("n", ssd_blk(g, 2)))
                items.append(("n", ssd_blk(g, 3)))
                items.append(("n", ssd_blk(g, 4)))
                if ps == 3:
                    items.append(("n", ssd_blk(g, 5)))
        else:
            for g in range(8):
                for k in range(6):
                    items.append(("n", ssd_blk(g, k)))
            for j in range(4):
                items.append(("w", j))
            for g in range(8):
                for k in range(4):
                    items.append(("n", gm_blk(g, k)))
            for d in range(16):
                for k in range(4):
                    items.append(("n", br_blk(d, k)))
            for d in range(16):
                items.append(("n", out_blk(d)))
    pos = 0
    plan = []
    for kind, idx in items:
        if kind == "w":
            pos = (pos + 3) // 4 * 4
            plan.append((kind, idx, pos, 4))
            pos += 4
        else:
            plan.append((kind, idx, pos, 1))
            pos += 1
    return plan


class _Stop(Exception):
    pass


DEBUG_STOP = None


def _cp(k):
    if DEBUG_STOP is not None and DEBUG_STOP == k:
        raise _Stop()


def build_program():
    order = _build(None)
    return _build(stream_plan(order))


def _build(plan):
    dry = plan is None
    order = []
    nc = bass.Bass("TRN2", target_bir_lowering=False)
    x_all = nc.dram_tensor("x_all", [8, 128, 8192], F32, kind="ExternalInput").ap()
    wn = nc.dram_tensor("wn", [NBLK, 128, 2048], F32, kind="ExternalInput").ap()
    wv = nc.dram_tensor("wv", [4, 128, 8192], F32, kind="ExternalInput").ap()
    wdt_d = nc.dram_tensor("wdt", [128, 512], F32, kind="ExternalInput").ap()
    cols_d = nc.dram_tensor("cols", [128, NCOL], F32, kind="ExternalInput").ap()
    rows_d = nc.dram_tensor("rows", [128, 64], F32, kind="ExternalInput").ap()
    bsb_d = nc.dram_tensor("bsb", [128, 1024], F32, kind="ExternalInput").ap()
    wst_d = nc.dram_tensor("wst", [128, 1024], F32, kind="ExternalInput").ap()
    out_d = nc.dram_tensor("out", [4, 128, 8192], F32, kind="ExternalOutput").ap()

    with ExitStack() as es:
        fw = FW(nc, es)

        def sb(name, shape, dt):
            return es.enter_context(nc.sbuf_tensor("sb_" + name, shape, dt))

        ring = sb("ring", [128, NSLOT * 2048], BF16)
        B_ring = [Buf() for _ in range(NSLOT)]
        hT = sb("hT", [128, 16, 512], BF16)
        B_hT = [Buf() for _ in range(16)]
        vm = sb("vm", [128, 8192], BF16)
        vn = vm[:].rearrange("p (c n) -> p c n", c=4)
        mg = vm[:].rearrange("p (k t) -> p k t", k=16)
        B_vm = [Buf() for _ in range(4)]
        ybuf = sb("ybuf", [128, 8192], F32)
        yT = ybuf[:].bitcast(BF16).rearrange("p (b t) -> p b t", t=512)
        xnew = ybuf[:].rearrange("p (k t) -> p k t", k=16)
        B_y = [Buf() for _ in range(32)]
        cols = sb("cols", [128, NCOL], F32)
        B_cols = Buf()
        rows = sb("rows", [128, 64], F32)
        B_rows = Buf()
        ident = sb("ident", [128, 128], BF16)
        tri = sb("tri", [128, 128], BF16)
        ones = sb("ones", [128, 128], BF16)
        maskneg = sb("maskneg", [128, 128], F32)
        sel = sb("sel", [128, 4, 128], BF16)
        WgT = sb("WgT", [128, 8, 128], BF16)
        Kc = sb("Kc", [128, 16, 128], F32)
        wdt = sb("wdt", [128, 16, 32], BF16)
        negA = sb("negA", [128, 32], F32)
        B_const = Buf()
        H = sb("H", [128, 8, 256], F32)
        B_H = [Buf() for _ in range(8)]
        Hpad = sb("Hpad", [128, 2, 4, 4, 128], BF16)
        B_Hpad = [[Buf() for _ in range(4)] for _ in range(2)]
        halo = sb("halo", [128, 32, 3], F32)
        B_halo = [Buf() for _ in range(32)]
        xt = sb("xt", [128, 3, 512], F32)
        B_xt = [Buf() for _ in range(3)]
        sq = sb("sq", [128, 2, 512], BF16)
        B_sq = [Buf() for _ in range(2)]
        sqp = sb("sqp", [128, 2, 512], BF16)
        B_sqp = [Buf() for _ in range(2)]
        xTg = sb("xTg", [128, 2, 2, 512], BF16)
        B_xTg = [[Buf() for _ in range(2)] for _ in range(2)]
        BTg = sb("BTg", [128, 2, 512], BF16)
        B_BTg = [Buf() for _ in range(2)]
        CTg = sb("CTg", [128, 2, 512], BF16)
        B_CTg = [Buf() for _ in range(2)]
        szb = sb("szb", [128, 2, 2, 512], BF16)
        B_szb = [[Buf() for _ in range(2)] for _ in range(2)]
        xpre = sb("xpre", [128, 3, 515], F32)
        B_xpre = [Buf() for _ in range(3)]
        btok = sb("btok", [128, 4, 128], BF16)
        B_btok = [Buf() for _ in range(4)]
        xdtpad = sb("xdtpad", [128, 4, 4, 128], BF16)
        B_xdt = [Buf() for _ in range(4)]
        xdec = sb("xdec", [128, 4, 256], BF16)
        B_xdec = [Buf() for _ in range(4)]
        scT = sb("scT", [128, 4, 512], BF16)
        B_scT = [Buf() for _ in range(4)]
        CE = sb("CE", [128, 4, 512], BF16)
        B_CE = [Buf() for _ in range(4)]
        acT = sb("acT", [128, 2, 512], BF16)
        B_acT = Buf()
        NT32 = 15
        t32 = sb("t32", [128, NT32, 512], F32)
        B_t32 = [Buf() for _ in range(NT32)]
        dtt = sb("dtt", [128, 4, 32], F32)
        adt = sb("adt", [128, 4, 32], F32)
        adt_hi = sb("adt_hi", [128, 4, 32], BF16)
        adt_lo = sb("adt_lo", [128, 4, 32], BF16)
        acum = sb("acum", [128, 4, 32], F32)
        aend = sb("aend", [128, 4, 32], F32)
        w2 = sb("w2", [128, 4, 32], F32)
        edec = sb("edec", [128, 4, 32], F32)
        dtmp = sb("dtmp", [128, 4, 32], F32)
        B_dt = Buf()
        B_dtw = Buf()
        B_adt = Buf()
        st1 = sb("st1", [128, 4, 4], F32)
        st2 = sb("st2", [128, 4, 4], F32)
        stm = sb("stm", [128, 8, 4], F32)
        B_st = Buf()

        pb = [es.enter_context(nc.psum_tensor("pb%d" % i, [128, 512], F32)) if i != 3 else None for i in range(8)]
        p3t = es.enter_context(nc.psum_tensor("p3t", [128, 512], BF16))
        B_pb = [Buf(excl=True) for _ in range(8)]

        class Rot:
            def __init__(self, lst):
                self.lst = lst
                self.i = 0

            def next(self):
                v = self.lst[self.i % len(self.lst)]
                self.i += 1
                return v

        wstate = {"next_issue": 0, "next_use": 0}
        k_ring = [fw.newsem("ring%d" % i) for i in range(NSLOT)]

        def ws_issue_upto(limit_pos):
            while wstate["next_issue"] < len(plan):
                kind, idx, pos, ln = plan[wstate["next_issue"]]
                if pos + ln > limit_pos:
                    break
                s0 = pos % NSLOT
                bufs = [B_ring[s0 + i] for i in range(ln)]
                dst = ring[:, s0 * 2048:(s0 + ln) * 2048]
                src = wn[idx] if kind == "n" else wv[idx]
                fw.issue("pool", lambda e, dst=dst, src=src: e.dma_start(out=dst, in_=src),
                         writes=bufs, dma_key=k_ring[s0])
                wstate["next_issue"] += 1

        def ws_next(kind, idx):
            if dry:
                order.append((kind, idx))
                ln = 4 if kind == "w" else 1
                return ring[:, 0:ln * 2048].rearrange("p (k c) -> p k c", k=16), [B_ring[i] for i in range(ln)]
            k, i, pos, ln = plan[wstate["next_use"]]
            assert (k, i) == (kind, idx), ((k, i), (kind, idx))
            wstate["next_use"] += 1
            ws_issue_upto(pos + NSLOT)
            s0 = pos % NSLOT
            bufs = [B_ring[s0 + j] for j in range(ln)]
            if kind == "n":
                view = ring[:, s0 * 2048:(s0 + 1) * 2048].rearrange("p (k c) -> p k c", k=16)
            else:
                view = ring[:, s0 * 2048:(s0 + 4) * 2048].rearrange("p (k c) -> p k c", k=16)
            return view, bufs

        k_c = fw.newsem("kconst0")
        k_c1 = fw.newsem("kconst1")
        k_c2 = fw.newsem("kconst2")
        k_c3 = fw.newsem("kconst3")
        fw.issue("sp", lambda e: e.dma_start(out=cols[:], in_=cols_d[:, :]), writes=[B_cols], dma_key=k_c)
        fw.issue("sp", lambda e: e.dma_start(out=rows[:], in_=rows_d[:, :]), writes=[B_rows], dma_key=k_c1)
        rsW = t32[:, 4:6, :].rearrange("p a (g t) -> p (a g) t", g=4)
        bs_bc = t32[:, 0:2, :].rearrange("p a (g t) -> p (a g) t", g=4)
        ws32 = t32[:, 2:4, :].rearrange("p a (g t) -> p (a g) t", g=4)
        fw.issue("sp", lambda e: e.dma_start(out=t32[:, 0:2, :].rearrange("p a t -> p (a t)"), in_=bsb_d[:, :]),
                 writes=[B_t32[0], B_t32[1]], dma_key=k_c2)
        fw.issue("sp", lambda e: e.dma_start(out=t32[:, 2:4, :].rearrange("p a t -> p (a t)"), in_=wst_d[:, :]),
                 writes=[B_t32[2], B_t32[3]], dma_key=k_c3)
        k_wdt = fw.newsem("kwdt")
        B_wdt = Buf()
        fw.issue("pool", lambda e: e.dma_start(out=wdt[:].rearrange("p k c -> p (k c)"), in_=wdt_d[:, :]),
                 writes=[B_wdt], dma_key=k_wdt)
        fw.issue("pool", lambda e: e.memset(ident[:], 1.0), writes=[B_const])
        fw.issue("pool", lambda e: e.affine_select(out=ident[:], in_=ident[:], pattern=[[-1, 128]],
                                                   compare_op=ALU.is_equal, fill=0.0, base=0, channel_multiplier=1),
                 writes=[B_const])
        fw.issue("pool", lambda e: e.memset(tri[:], 1.0), writes=[B_const])
        fw.issue("pool", lambda e: e.affine_select(out=tri[:], in_=tri[:], pattern=[[1, 128]],
                                                   compare_op=ALU.is_ge, fill=0.0, base=0, channel_multiplier=-1),
                 writes=[B_const])
        fw.issue("pool", lambda e: e.memset(maskneg[:], 0.0), writes=[B_const])
        fw.issue("pool", lambda e: e.affine_select(out=maskneg[:], in_=maskneg[:], pattern=[[1, 128]],
                                                   compare_op=ALU.is_ge, fill=NEG, base=0, channel_multiplier=-1),
                 writes=[B_const])
        fw.issue("pool", lambda e: e.memset(ones[:], 1.0), writes=[B_const])
        fw.issue("pool", lambda e: e.memset(sel[:], 1.0), writes=[B_const])
        for hl in range(4):
            fw.issue("pool", lambda e, hl=hl: e.affine_select(out=sel[:, hl, :], in_=sel[:, hl, :], pattern=[[0, 128]],
                                                              compare_op=ALU.is_equal, fill=0.0, base=-hl,
                                                              channel_multiplier=1), writes=[B_const])
        fw.issue("pool", lambda e: e.memset(H[:], 0.0), writes=B_H)
        fw.issue("pool", lambda e: e.memset(Hpad[:], 0.0), writes=B_Hpad[0] + B_Hpad[1])
        fw.issue("pool", lambda e: e.memset(halo[:], 0.0), writes=B_halo)
        fw.issue("pool", lambda e: e.memset(xdtpad[:], 0.0), writes=B_xdt)
        fw.issue("pool", lambda e: e.memset(acT[:], 0.0), writes=[B_acT])
        fw.issue("dve", lambda e: e.tensor_copy(out=WgT[:], in_=ws32), reads=[B_t32[2], B_t32[3]], writes=[B_const])
        for g in range(8):
            fw.issue("pool", lambda e, g=g: e.affine_select(out=WgT[:, g, :], in_=WgT[:, g, :], pattern=[[1, 128]],
                                                            compare_op=ALU.is_ge, fill=0.0, base=0,
                                                            channel_multiplier=-1), writes=[B_const])
        for half in range(2):
            fw.group("pe", [lambda e, g=g, half=half: e.matmul(pb[half][:, (g % 4) * 128:(g % 4 + 1) * 128], lhsT=ones[:],
                                                              rhs=WgT[:, g, :], start=True, stop=True)
                            for g in range(half * 4, half * 4 + 4)], reads=[B_const], writes=[B_pb[half]])
            fw.issue("act", lambda e, half=half: e.copy(out=t32[:, 4 + half, :], in_=pb[half][:]), reads=[B_pb[half]], writes=[B_t32[4 + half]])
        for cb in range(16):
            fw.issue("dve", lambda e, cb=cb: e.scalar_tensor_tensor(out=Kc[:, cb, :], in0=rsW[:, cb // 2, :],
                                                                    scalar=cols[:, C_LNB + cb:C_LNB + cb + 1],
                                                                    in1=bs_bc[:, cb // 2, :], op0=ALU.mult, op1=ALU.add),
                     reads=[B_const, B_cols, B_t32[0], B_t32[1], B_t32[4], B_t32[5]], writes=[B_const])
        fw.issue("act", lambda e: e.activation(out=negA[:], in_=rows[:, 32:64], func=AF.Exp), reads=[B_rows], writes=[B_const])
        fw.issue("dve", lambda e: e.tensor_scalar(out=negA[:], in0=negA[:], scalar1=-1.0, scalar2=None, op0=ALU.mult),
                 reads=[B_const], writes=[B_const])

        k_xt = [fw.newsem("kxt%d" % i) for i in range(3)]
        xt_rot = Rot([0, 1, 2])
        sq_rot = Rot([0, 1])
        sqp_rot = Rot([0, 1])
        k_out = {to: fw.newsem("kout%d" % to) for to in (7, 8, 9)}

        def load_x(ps, kc):
            i = xt_rot.next()
            fw.issue("sp", lambda e: e.dma_start(out=xt[:, i, :], in_=x_all[ps][:, kc * 512:(kc + 1) * 512]),
                     writes=[B_xt[i]], dma_key=k_xt[i])
            return i

        def rstd_from_psum(bank, scale, tdst):
            fw.issue("act", lambda e: e.activation(out=t32[:, tdst, :], in_=pb[bank][:], func=AF.Sqrt, bias=EPS, scale=scale),
                     reads=[B_pb[bank]], writes=[B_t32[tdst]])
            fw.issue("dve", lambda e: e.reciprocal(out=t32[:, tdst, :], in_=t32[:, tdst, :]),
                     reads=[B_t32[tdst]], writes=[B_t32[tdst]])

        hT_alt = vm[:].rearrange("p (k t) -> p k t", k=16)
        hbufs = [hT[:], hT_alt]
        B_hTs = [B_hT, [B_vm[kc // 4] for kc in range(16)]]
        cur = {"h": hbufs[0], "B_h": B_hTs[0]}

        def hsel(ps):
            return ps % 2 if ps < 4 else 0

        def inproj(view, wbufs, bank, hi=None):
            hc = cur["h"] if hi is None else hbufs[hi]
            Bh = cur["B_h"] if hi is None else B_hTs[hi]
            fw.group("pe", [lambda e, kc=kc: e.matmul(pb[bank][:], lhsT=view[:, kc, :], rhs=hc[:, kc, :],
                                                      start=(kc == 0), stop=(kc == 15)) for kc in range(16)],
                     reads=list(wbufs) + list(dict.fromkeys(Bh)), writes=[B_pb[bank]])

        rot = Rot([0, 1, 2])
        rot_main = rot
        rot_pre = Rot([0, 1, 2, 5, 6])
        a0_done = set()
        a1_done = set()

        def A_units(ps_, g):
            main_ = ps_ >= 4
            last_pre_ = ps_ == 3
            hi = hsel(ps_)
            par = g % 2
            rot = rot_pre if ps_ < 4 else rot_main

            def a_z(j):
                view, wb = ws_next("n", ssd_blk(g, j))
                bank = rot.next()
                inproj(view, wb, bank, hi)
                fw.issue("act", lambda e: e.activation(out=szb[:, par, j, :], in_=pb[bank][:], func=AF.Silu),
                         reads=[B_pb[bank]], writes=[B_szb[par][j]])

            def a_x(j):
                view, wb = ws_next("n", ssd_blk(g, 2 + j))
                bank = rot.next()
                inproj(view, wb, bank, hi)
                conv_block(bank, 2 * g + j, xTg[:, par, j, :], [B_xTg[par][j]])

            def a_b():
                view, wb = ws_next("n", ssd_blk(g, 4))
                bank = rot.next()
                inproj(view, wb, bank, hi)
                conv_block(bank, 16 + g, BTg[:, par, :], [B_BTg[par]])

            def a_c():
                view, wb = ws_next("n", ssd_blk(g, 5))
                bank = rot.next()
                inproj(view, wb, bank, hi)
                conv_block(bank, 24 + g, CTg[:, par, :], [B_CTg[par]], only_halo=not main_)

            units = []
            if main_:
                units += [lambda j=j: a_z(j) for j in range(2)]
            units += [lambda j=j: a_x(j) for j in range(2)]
            units.append(a_b)
            if main_ or last_pre_:
                units.append(a_c)
            return units

        def ht_units(psn, split=False):
            hb = hbufs[hsel(psn)]
            Bh = B_hTs[hsel(psn)]
            units = []

            pend = []

            def sq_part(k0):
                mm_part()
                for kc in range(k0, k0 + 2):
                    xi = load_x(psn, kc)
                    si = sqp_rot.next()
                    fw.issue("act", lambda e: e.activation(out=sqp[:, si, :], in_=xt[:, xi, :], func=AF.Square),
                             reads=[B_xt[xi]], writes=[B_sqp[si]])
                    pend.append((kc, si))

            def mm_part():
                while pend:
                    kc, si = pend.pop(0)
                    fw.issue("pe", lambda e: e.matmul(pb[4][:], lhsT=ones[:], rhs=sqp[:, si, :], start=(kc == 0), stop=(kc == 15)),
                             reads=[B_sqp[si], B_const], writes=[B_pb[4]])
                    if kc == 15:
                        rstd_from_psum(4, 1.0 / D, 10)

            def sc_part(k0):
                for kc in range(k0, k0 + 4):
                    xi = load_x(psn, kc)
                    fw.issue("dve", lambda e: e.scalar_tensor_tensor(out=hb[:, kc, :], in0=xt[:, xi, :],
                                                                     scalar=cols[:, C_NW + kc:C_NW + kc + 1],
                                                                     in1=t32[:, 10, :], op0=ALU.mult, op1=ALU.mult),
                             reads=[B_xt[xi], B_cols, B_t32[10]], writes=[Bh[kc]])

            units += [lambda k0=k0: sq_part(k0) for k0 in range(0, 16, 2)]
            units.append(mm_part)
            if split:
                return units, [lambda k0=k0: sc_part(k0) for k0 in (0, 4, 8, 12)]
            units += [lambda k0=k0: sc_part(k0) for k0 in (0, 4, 8, 12)]
            return units

        def dt_phase(psn):
            hc = hbufs[hsel(psn)]
            Bh = list(dict.fromkeys(B_hTs[hsel(psn)]))
            for c in range(4):
                fw.group("pe", [lambda e, kc=kc: e.matmul(pb[6][:, c * 32:(c + 1) * 32], lhsT=hc[:, kc, c * 128:(c + 1) * 128],
                                                          rhs=wdt[:, kc, :], start=(kc == 0), stop=(kc == 15))
                                for kc in range(16)], reads=Bh + [B_wdt], writes=[B_pb[6]])
            p6 = pb[6][:, 0:128].rearrange("p (c h) -> p c h", c=4)
            dtb_bc = rows[:, 0:32].unsqueeze(1).broadcast_to([128, 4, 32])
            negA_bc = negA[:].unsqueeze(1).broadcast_to([128, 4, 32])
            fw.issue("dve", lambda e: e.tensor_tensor(out=dtmp[:], in0=p6, in1=dtb_bc, op=ALU.add),
                     reads=[B_pb[6], B_rows], writes=[B_dtw])
            fw.issue("act", lambda e: e.activation(out=dtmp[:], in_=dtmp[:], func=AF.Exp), reads=[B_dtw], writes=[B_dtw])
            fw.issue("act", lambda e: e.activation(out=dtt[:], in_=dtmp[:], func=AF.Ln, bias=1.0, scale=1.0),
                     reads=[B_dtw], writes=[B_dt])
            fw.issue("dve", lambda e: e.tensor_tensor(out=adt[:], in0=dtt[:], in1=negA_bc, op=ALU.mult),
                     reads=[B_dt, B_const], writes=[B_adt])
            fw.issue("dve", lambda e: e.tensor_copy(out=adt_hi[:], in_=adt[:]), reads=[B_adt], writes=[B_adt])
            fw.issue("dve", lambda e: e.tensor_tensor(out=adt_lo[:], in0=adt[:], in1=adt_hi[:], op=ALU.subtract),
                     reads=[B_adt], writes=[B_adt])
            mms = []
            for c in range(4):
                mms.append(lambda e, c=c: e.matmul(pb[6][:, 128 + c * 32:128 + (c + 1) * 32], lhsT=tri[:], rhs=adt_hi[:, c, :], start=True, stop=False))
                mms.append(lambda e, c=c: e.matmul(pb[6][:, 128 + c * 32:128 + (c + 1) * 32], lhsT=tri[:], rhs=adt_lo[:, c, :], start=False, stop=True))
                mms.append(lambda e, c=c: e.matmul(pb[6][:, 256 + c * 32:256 + (c + 1) * 32], lhsT=ones[:], rhs=adt_hi[:, c, :], start=True, stop=False))
                mms.append(lambda e, c=c: e.matmul(pb[6][:, 256 + c * 32:256 + (c + 1) * 32], lhsT=ones[:], rhs=adt_lo[:, c, :], start=False, stop=True))
            fw.group("pe", mms, reads=[B_adt, B_const], writes=[B_pb[6]])
            fw.issue("act", lambda e: e.copy(out=acum[:].rearrange("p c h -> p (c h)"), in_=pb[6][:, 128:256]),
                     reads=[B_pb[6]], writes=[B_dt])
            fw.issue("act", lambda e: e.copy(out=aend[:].rearrange("p c h -> p (c h)"), in_=pb[6][:, 256:384]),
                     reads=[B_pb[6]], writes=[B_dt])
            fw.issue("dve", lambda e: e.tensor_tensor(out=dtmp[:], in0=aend[:], in1=acum[:], op=ALU.subtract),
                     reads=[B_dt], writes=[B_dtw])
            fw.issue("act", lambda e: e.activation(out=dtmp[:], in_=dtmp[:], func=AF.Exp), reads=[B_dtw], writes=[B_dtw])
            fw.issue("dve", lambda e: e.tensor_tensor(out=w2[:], in0=dtmp[:], in1=dtt[:], op=ALU.mult),
                     reads=[B_dtw, B_dt], writes=[B_dt])
            fw.issue("act", lambda e: e.activation(out=edec[:], in_=aend[:], func=AF.Exp), reads=[B_dt], writes=[B_dt])

        xpre_rot = Rot([0, 1, 2])

        def conv_block(bank, ci, dst_ap, dst_bufs, only_halo=False):
            xi = xpre_rot.next()
            fw.issue("dve", lambda e: e.tensor_copy(out=xpre[:, xi, 0:3], in_=halo[:, ci, :]),
                     reads=[B_halo[ci]], writes=[B_xpre[xi]])
            fw.issue("act", lambda e: e.copy(out=xpre[:, xi, 3:515], in_=pb[bank][:]),
                     reads=[B_pb[bank]], writes=[B_xpre[xi]])
            fw.issue("dve", lambda e: e.tensor_copy(out=halo[:, ci, :], in_=xpre[:, xi, 512:515]),
                     reads=[B_xpre[xi]], writes=[B_halo[ci]])
            if only_halo:
                return
            ct = 12 + (ci % 2)
            fw.issue("act", lambda e: e.activation(out=t32[:, ct, :], in_=xpre[:, xi, 0:512], func=AF.Identity,
                                                   bias=cols[:, C_CB + ci:C_CB + ci + 1],
                                                   scale=cols[:, C_CW + ci:C_CW + ci + 1]),
                     reads=[B_xpre[xi], B_cols], writes=[B_t32[ct]])
            for k in range(1, 4):
                fw.issue("dve", lambda e, k=k: e.scalar_tensor_tensor(out=t32[:, ct, :], in0=xpre[:, xi, k:k + 512],
                                                                      scalar=cols[:, C_CW + 32 * k + ci:C_CW + 32 * k + ci + 1],
                                                                      in1=t32[:, ct, :], op0=ALU.mult, op1=ALU.add),
                         reads=[B_xpre[xi], B_t32[ct]], writes=[B_t32[ct]])
            fw.issue("act", lambda e: e.activation(out=dst_ap, in_=t32[:, ct, :], func=AF.Silu),
                     reads=[B_t32[ct]], writes=dst_bufs)

        def bc3(ap2, n):
            return ap2.unsqueeze(2).broadcast_to([128, ap2.shape[1], n])

        def do_pass(ps):
            main = ps >= 4
            last_pre = ps == 3
            if ps == 4:
                flag = cols[:, C_FLAG:C_FLAG + 1]
                fw.issue("dve", lambda e: e.tensor_scalar(out=H[:].rearrange("p g n -> p (g n)"), in0=H[:].rearrange("p g n -> p (g n)"),
                                                          scalar1=flag, scalar2=None, op0=ALU.mult),
                         reads=[B_cols], writes=B_H)
                fw.issue("dve", lambda e: e.tensor_scalar(out=halo[:].rearrange("p g n -> p (g n)"), in0=halo[:].rearrange("p g n -> p (g n)"),
                                                          scalar1=flag, scalar2=None, op0=ALU.mult),
                         reads=[B_cols], writes=B_halo)

            cur["h"] = hbufs[hsel(ps)]
            cur["B_h"] = B_hTs[hsel(ps)]
            hcur = cur["h"]
            Bhcur = list(dict.fromkeys(cur["B_h"]))
            dt_early = ps >= 5
            _cp(ps * 10 + 3)
            p3b = p3t[:]

            def ssd_steps(g):
                par = g % 2
                if main:
                    mms = []
                    for c in range(4):
                        mms.append(lambda e, c=c: e.matmul(pb[7][0:4, c * 128:(c + 1) * 128], lhsT=adt_hi[:, c, 4 * g:4 * g + 4], rhs=tri[:], start=True, stop=False))
                        mms.append(lambda e, c=c: e.matmul(pb[7][0:4, c * 128:(c + 1) * 128], lhsT=adt_lo[:, c, 4 * g:4 * g + 4], rhs=tri[:], start=False, stop=True))
                    fw.group("pe", mms, reads=[B_adt, B_const], writes=[B_pb[7]])
                    fw.issue("act", lambda e: e.copy(out=t32[0:4, 10, :], in_=pb[7][0:4, :]), reads=[B_pb[7]], writes=[B_t32[10]])
                    fw.issue("dve", lambda e: e.tensor_copy(out=acT[0:4, 0, :], in_=t32[0:4, 10, :]), reads=[B_t32[10]], writes=[B_acT])
                    fw.issue("dve", lambda e: e.tensor_tensor(out=acT[0:4, 1, :], in0=t32[0:4, 10, :], in1=acT[0:4, 0, :], op=ALU.subtract),
                             reads=[B_t32[10], B_acT], writes=[B_acT])
                for c in range(4):
                    cs = slice(c * 128, (c + 1) * 128)
                    trs = [lambda e, j=j: e.transpose(p3b[:, j * 128:(j + 1) * 128], xTg[:, par, j, cs], ident[:]) for j in range(2)]
                    trs.append(lambda e: e.transpose(p3b[:, 256:384], BTg[:, par, cs], ident[:]))
                    fw.group("pe", trs, reads=[B_xTg[par][0], B_xTg[par][1], B_BTg[par], B_const], writes=[B_pb[3]])
                    fw.issue("dve", lambda e: e.tensor_copy(out=btok[:, c, :], in_=p3b[:, 256:384]), reads=[B_pb[3]], writes=[B_btok[c]])
                    xtk = p3b[:, 0:256].rearrange("p (h q) -> p h q", h=4)
                    fw.issue("dve", lambda e: e.tensor_tensor(out=xdec[:, c, :].rearrange("p (h q) -> p h q", h=4), in0=xtk,
                                                              in1=bc3(w2[:, c, 4 * g:4 * g + 4], 64), op=ALU.mult),
                             reads=[B_pb[3], B_dt], writes=[B_xdec[c]])
                    if main:
                        for q in range(2):
                            fw.issue("dve", lambda e: e.tensor_tensor(out=xdtpad[:, c, q::2, q * 64:(q + 1) * 64], in0=xtk[:, q::2, :],
                                                                      in1=bc3(dtt[:, c, 4 * g + q:4 * g + 4:2], 64), op=ALU.mult),
                                     reads=[B_pb[3], B_dt], writes=[B_xdt[c]])
                    yield
                Hg = H[:, g, :].rearrange("p (h q) -> p h q", h=4)
                if main:
                    for q in range(2):
                        fw.issue("act", lambda e: e.copy(out=Hpad[:, par, 0, q::2, q * 64:(q + 1) * 64], in_=Hg[:, q::2, :]),
                                 reads=[B_H[g]], writes=[B_Hpad[par][0]])
                for c in range(4):
                    hs = slice((c % 2) * 256, (c % 2) * 256 + 256)
                    fw.issue("pe", lambda e: e.matmul(pb[7][:, hs], lhsT=btok[:, c, :], rhs=xdec[:, c, :], start=True, stop=True),
                             reads=[B_btok[c], B_xdec[c]], writes=[B_pb[7]])
                    fw.issue("dve", lambda e: e.tensor_tensor(out=Hg, in0=Hg, in1=bc3(edec[:, c, 4 * g:4 * g + 4], 64), op=ALU.mult),
                             reads=[B_dt], writes=[B_H[g]])
                    fw.issue("dve", lambda e: e.tensor_tensor(out=H[:, g, :], in0=H[:, g, :], in1=pb[7][:, hs], op=ALU.add),
                             reads=[B_pb[7]], writes=[B_H[g]])
                    if main and c < 3:
                        for q in range(2):
                            fw.issue("act", lambda e: e.copy(out=Hpad[:, par, c + 1, q::2, q * 64:(q + 1) * 64], in_=Hg[:, q::2, :]),
                                     reads=[B_H[g]], writes=[B_Hpad[par][c + 1]])
                    if c % 2 == 1:
                        yield
                if main:
                    fw.group("pe", [lambda e, c=c: e.matmul(pb[4][:, c * 128:(c + 1) * 128], lhsT=BTg[:, par, c * 128:(c + 1) * 128],
                                                            rhs=CTg[:, par, c * 128:(c + 1) * 128], start=True, stop=True) for c in range(4)],
                             reads=[B_BTg[par], B_CTg[par]], writes=[B_pb[4]])
                    fw.issue("act", lambda e: e.copy(out=t32[:, 9, :], in_=pb[4][:]), reads=[B_pb[4]], writes=[B_t32[9]])
                    for hl in range(4):
                        h = 4 * g + hl
                        rb = rot.next()
                        fw.group("pe", [lambda e: e.matmul(pb[rb][:], lhsT=sel[:, hl, :], rhs=acT[:, 0, :], start=True, stop=False),
                                        lambda e: e.matmul(pb[rb][:], lhsT=sel[:, hl, :], rhs=acT[:, 1, :], start=False, stop=True)],
                                 reads=[B_acT, B_const], writes=[B_pb[rb]])
                        ts = 0 + (hl % 2)
                        te = 2 + (hl % 2)
                        for c in range(4):
                            fw.issue("dve", lambda e: e.scalar_tensor_tensor(
                                out=t32[:, ts, c * 128:(c + 1) * 128], in0=pb[rb][:, c * 128:(c + 1) * 128],
                                scalar=acum[:, c, h:h + 1], in1=maskneg[:], op0=ALU.subtract, op1=ALU.add),
                                reads=[B_pb[rb], B_dt, B_const], writes=[B_t32[ts]])
                        fw.issue("act", lambda e: e.activation(out=t32[:, te, :], in_=pb[rb][:], func=AF.Exp),
                                 reads=[B_pb[rb]], writes=[B_t32[te]])
                        fw.issue("act", lambda e: e.activation(out=t32[:, ts, :], in_=t32[:, ts, :], func=AF.Exp),
                                 reads=[B_t32[ts]], writes=[B_t32[ts]])
                        fw.issue("dve", lambda e: e.tensor_tensor(out=CE[:, hl, :], in0=t32[:, te, :], in1=CTg[:, par, :], op=ALU.mult),
                                 reads=[B_t32[te], B_CTg[par]], writes=[B_CE[hl]])
                        fw.issue("dve", lambda e: e.tensor_tensor(out=scT[:, hl, :], in0=t32[:, ts, :], in1=t32[:, 9, :], op=ALU.mult),
                                 reads=[B_t32[ts], B_t32[9]], writes=[B_scT[hl]])
                        yield

                if main:
                    for c in range(4):
                        cs = slice(c * 128, (c + 1) * 128)
                        for j in range(2):
                            mms = []
                            for hl in (2 * j, 2 * j + 1):
                                mms.append(lambda e, hl=hl: e.matmul(pb[5 + j][:, cs], lhsT=xdtpad[:, c, hl, :], rhs=scT[:, hl, cs],
                                                                     start=(hl == 2 * j), stop=False))
                            for hl in (2 * j, 2 * j + 1):
                                mms.append(lambda e, hl=hl: e.matmul(pb[5 + j][:, cs], lhsT=Hpad[:, par, c, hl, :], rhs=CE[:, hl, cs],
                                                                     start=False, stop=(hl == 2 * j + 1)))
                            fw.group("pe", mms, reads=[B_xdt[c], B_scT[2 * j], B_scT[2 * j + 1], B_Hpad[par][c], B_CE[2 * j], B_CE[2 * j + 1]],
                                     writes=[B_pb[5 + j]])
                        if c % 2 == 1:
                            yield

                if main:
                    for j in range(2):
                        blk = 2 * g + j
                        ty = 4 + j
                        fw.issue("dve", lambda e: e.scalar_tensor_tensor(out=t32[:, ty, :], in0=xTg[:, par, j, :],
                                                                         scalar=cols[:, C_DS + blk:C_DS + blk + 1],
                                                                         in1=pb[5 + j][:], op0=ALU.mult, op1=ALU.add),
                                 reads=[B_xTg[par][j], B_cols, B_pb[5 + j]], writes=[B_t32[ty]])
                        fw.issue("dve", lambda e: e.tensor_tensor(out=t32[:, ty, :], in0=t32[:, ty, :], in1=szb[:, par, j, :], op=ALU.mult),
                                 reads=[B_t32[ty], B_szb[par][j]], writes=[B_t32[ty]])
                        si = sq_rot.next()
                        fw.issue("act", lambda e: e.activation(out=sq[:, si, :], in_=t32[:, ty, :], func=AF.Square),
                                 reads=[B_t32[ty]], writes=[B_sq[si]])
                        fw.issue("pe", lambda e: e.matmul(pb[7][:], lhsT=ones[:], rhs=sq[:, si, :], start=(j == 0), stop=(j == 1)),
                                 reads=[B_sq[si], B_const], writes=[B_pb[7]])
                    rstd_from_psum(7, 1.0 / 256.0, 6)
                    for j in range(2):
                        blk = 2 * g + j
                        ty = 4 + j
                        fw.issue("dve", lambda e: e.scalar_tensor_tensor(out=yT[:, 16 + blk, :], in0=t32[:, ty, :],
                                                                         scalar=cols[:, C_SNW + blk:C_SNW + blk + 1],
                                                                         in1=t32[:, 6, :], op0=ALU.mult, op1=ALU.mult),
                                 reads=[B_t32[ty], B_cols, B_t32[6]], writes=[B_y[16 + blk]])
                    yield

            def v_unit(j):
                view, wb = ws_next("w", j)
                for c in range(4):
                    bank = rot.next()
                    fw.group("pe", [lambda e, kc=kc: e.matmul(pb[bank][:], lhsT=hcur[:, kc, c * 128:(c + 1) * 128],
                                                              rhs=view[:, kc, :], start=(kc == 0), stop=(kc == 15))
                                    for kc in range(16)], reads=list(wb) + Bhcur, writes=[B_pb[bank]])
                    fw.issue("act", lambda e: e.copy(out=vn[:, c, j * 512:(j + 1) * 512], in_=pb[bank][:]),
                             reads=[B_pb[bank]], writes=[B_vm[c]])
                    fw.issue("dve", lambda e: e.tensor_reduce(out=st1[:, c, j:j + 1], in_=pb[bank][:], axis=AX.X, op=ALU.add),
                             reads=[B_pb[bank]], writes=[B_st])
                    tq = 7 + (c % 2)
                    fw.issue("act", lambda e: e.activation(out=t32[:, tq, :], in_=pb[bank][:], func=AF.Square),
                             reads=[B_pb[bank]], writes=[B_t32[tq]])
                    fw.issue("dve", lambda e: e.tensor_reduce(out=st2[:, c, j:j + 1], in_=t32[:, tq, :], axis=AX.X, op=ALU.add),
                             reads=[B_t32[tq]], writes=[B_st])

            def norm_unit():
                fw.issue("dve", lambda e: e.tensor_reduce(out=stm[:, 0, :], in_=st1[:], axis=AX.X, op=ALU.add), reads=[B_st], writes=[B_st])
                fw.issue("dve", lambda e: e.tensor_reduce(out=stm[:, 1, :], in_=st2[:], axis=AX.X, op=ALU.add), reads=[B_st], writes=[B_st])
                fw.issue("dve", lambda e: e.tensor_scalar(out=stm[:, 2, :], in0=stm[:, 0, :], scalar1=1.0 / E, scalar2=None, op0=ALU.mult),
                         reads=[B_st], writes=[B_st])
                fw.issue("dve", lambda e: e.tensor_tensor(out=stm[:, 3, :], in0=stm[:, 2, :], in1=stm[:, 2, :], op=ALU.mult),
                         reads=[B_st], writes=[B_st])
                fw.issue("dve", lambda e: e.scalar_tensor_tensor(out=stm[:, 4, :], in0=stm[:, 1, :], scalar=1.0 / E, in1=stm[:, 3, :],
                                                                 op0=ALU.mult, op1=ALU.subtract), reads=[B_st], writes=[B_st])
                fw.issue("act", lambda e: e.activation(out=stm[:, 5, :], in_=stm[:, 4, :], func=AF.Sqrt, bias=EPS, scale=1.0),
                         reads=[B_st], writes=[B_st])
                fw.issue("dve", lambda e: e.reciprocal(out=stm[:, 5, :], in_=stm[:, 5, :]), reads=[B_st], writes=[B_st])
                fw.issue("dve", lambda e: e.scalar_tensor_tensor(out=stm[:, 6, :], in0=stm[:, 2, :], scalar=-1.0, in1=stm[:, 5, :],
                                                                 op0=ALU.mult, op1=ALU.mult), reads=[B_st], writes=[B_st])
                for c in range(4):
                    fw.issue("dve", lambda e: e.tensor_scalar(out=vn[:, c, :], in0=vn[:, c, :], scalar1=stm[:, 5, c:c + 1],
                                                              scalar2=stm[:, 6, c:c + 1], op0=ALU.mult, op1=ALU.add),
                             reads=[B_st], writes=[B_vm[c]])

            def cb_unit(cb):
                g = cb // 2
                j = cb % 2
                view, wb = ws_next("n", gm_blk(g, 2 + j))
                zb = rot.next()
                inproj(view, wb, zb)
                fw.issue("act", lambda e: e.activation(out=t32[:, 11, :], in_=pb[zb][:], func=AF.Silu),
                         reads=[B_pb[zb]], writes=[B_t32[11]])
                view, wb = ws_next("n", gm_blk(g, j))
                ub = rot.next()
                inproj(view, wb, ub)
                fw.issue("dve", lambda e: e.tensor_tensor(out=t32[:, 11, :], in0=t32[:, 11, :], in1=pb[ub][:], op=ALU.mult),
                         reads=[B_t32[11], B_pb[ub]], writes=[B_t32[11]])
                mb = rot.next()
                fw.group("pe", [lambda e, c=c: e.matmul(pb[mb][:, c * 128:(c + 1) * 128], lhsT=vn[:, c, cb * 128:(cb + 1) * 128],
                                                        rhs=WgT[:, g, :], start=True, stop=True) for c in range(4)],
                         reads=B_vm + [B_const], writes=[B_pb[mb]])
                fw.issue("dve", lambda e: e.scalar_tensor_tensor(
                    out=t32[:, 14, :].rearrange("p (c t) -> p c t", c=4), in0=pb[mb][:].rearrange("p (c t) -> p c t", c=4),
                    scalar=cols[:, C_LNW + cb:C_LNW + cb + 1], in1=Kc[:, cb, :].unsqueeze(1).broadcast_to([128, 4, 128]),
                    op0=ALU.mult, op1=ALU.add), reads=[B_pb[mb], B_cols, B_const], writes=[B_t32[14]])
                fw.issue("dve", lambda e: e.tensor_tensor(out=yT[:, cb, :], in0=t32[:, 11, :], in1=t32[:, 14, :], op=ALU.mult),
                         reads=[B_t32[11], B_t32[14]], writes=[B_y[cb]])

            if ps not in a0_done:
                for u in A_units(ps, 0):
                    u()
            if not dt_early:
                dt_phase(ps)
            gm = []
            if not main:
                gm += ht_units(ps + 1)
            if main:
                gm += [lambda j=j: v_unit(j) for j in range(4)]
                gm.append(norm_unit)
                gm += [lambda cb=cb: cb_unit(cb) for cb in range(16)]
            gquota = [2, 3, 3, 3, 3, 3, 3, 3] if main else [2, 2, 2, 2, 2, 2, 2, 2]
            for g in range(8):
                if g == 1:
                    _cp(ps * 10 + 4)
                if g == 0 and ps in a1_done:
                    nextA = []
                elif g < 7:
                    nextA = A_units(ps, g + 1)
                elif ps < 3:
                    nextA = A_units(ps + 1, 0)
                    a0_done.add(ps + 1)
                else:
                    nextA = []
                gq = gquota[g]
                k = 0
                for _ in ssd_steps(g):
                    if k % 2 == 0 and nextA:
                        nextA.pop(0)()
                    elif gm and gq > 0:
                        gm.pop(0)()
                        gq -= 1
                    elif nextA:
                        nextA.pop(0)()
                    k += 1
                while nextA:
                    nextA.pop(0)()
            while gm:
                gm.pop(0)()

            _cp(ps * 10 + 5)
            if not main:
                return

            _cp(ps * 10 + 7)
            grot = Rot([0, 1, 2])
            brot = Rot([5, 6, 7])
            pro_sq, pro_sc = (ht_units(ps + 1, split=True) if ps < 7 else ([], []))
            for d in range(16):
                if pro_sq and d >= 2:
                    pro_sq.pop(0)()
                gt = []
                for n in range(2):
                    view, wb = ws_next("n", br_blk(d, n))
                    bank = grot.next()
                    inproj(view, wb, bank)
                    tg = 4 + 2 * (d % 2) + n
                    fw.issue("act", lambda e, bank=bank, tg=tg, n=n, d=d: e.activation(out=t32[:, tg, :], in_=pb[bank][:], func=AF.Sigmoid,
                                                                                       bias=cols[:, C_BG + 16 * n + d:C_BG + 16 * n + d + 1], scale=1.0),
                             reads=[B_pb[bank], B_cols], writes=[B_t32[tg]])
                    gt.append(tg)
                bb = []
                for n in range(2):
                    view, wb = ws_next("n", br_blk(d, 2 + n))
                    bank = brot.next()
                    fw.group("pe", [lambda e, ec=ec, n=n, bank=bank, view=view: e.matmul(pb[bank][:], lhsT=view[:, ec, :], rhs=yT[:, 16 * n + ec, :],
                                                                                         start=(ec == 0), stop=(ec == 15)) for ec in range(16)],
                             reads=list(wb) + B_y[16 * n:16 * n + 16], writes=[B_pb[bank]])
                    bb.append(bank)
                for n in range(2):
                    fw.issue("dve", lambda e, n=n, tg=gt[n], bank=bb[n]: e.tensor_tensor(out=t32[:, tg, :], in0=t32[:, tg, :], in1=pb[bank][:], op=ALU.mult),
                             reads=[B_t32[gt[n]], B_pb[bb[n]]], writes=[B_t32[gt[n]]])
                fw.issue("dve", lambda e, d=d, g0=gt[0], g1=gt[1]: e.tensor_tensor(out=mg[:, d, :], in0=t32[:, g0, :], in1=t32[:, g1, :], op=ALU.add),
                         reads=[B_t32[gt[0]], B_t32[gt[1]]], writes=[B_vm[d // 4]])

            _cp(ps * 10 + 8)
            orot = Rot([0, 1, 2])
            pro = []
            if ps < 7:
                pro = pro_sq + pro_sc + [lambda: dt_phase(ps + 1)] + A_units(ps + 1, 0) + A_units(ps + 1, 1)
                a0_done.add(ps + 1)
                a1_done.add(ps + 1)
            for d in range(16):
                if pro and d >= 1:
                    pro.pop(0)()
                if pro and d >= 6:
                    pro.pop(0)()
                view, wb = ws_next("n", out_blk(d))
                bank = orot.next()
                fw.group("pe", [lambda e, dc=dc, bank=bank, view=view: e.matmul(pb[bank][:], lhsT=view[:, dc, :], rhs=mg[:, dc, :],
                                                                               start=(dc == 0), stop=(dc == 15)) for dc in range(16)],
                         reads=list(wb) + B_vm, writes=[B_pb[bank]])
                xi = load_x(ps, d)
                fw.issue("dve", lambda e, d=d, xi=xi, bank=bank: e.tensor_tensor(out=xnew[:, d, :], in0=xt[:, xi, :], in1=pb[bank][:], op=ALU.add),
                         reads=[B_xt[xi], B_pb[bank]], writes=[B_y[2 * d], B_y[2 * d + 1]])
                si = sq_rot.next()
                fw.issue("act", lambda e, d=d, si=si: e.activation(out=sq[:, si, :], in_=xnew[:, d, :], func=AF.Square),
                         reads=[B_y[2 * d], B_y[2 * d + 1]], writes=[B_sq[si]])
                fw.issue("pe", lambda e, d=d, si=si: e.matmul(pb[7][:], lhsT=ones[:], rhs=sq[:, si, :], start=(d == 0), stop=(d == 15)),
                         reads=[B_sq[si], B_const], writes=[B_pb[7]])
            while pro:
                pro.pop(0)()
            rstd_from_psum(7, 1.0 / D, 11)
            for d in range(16):
                to = 7 + (d % 3)
                fw.issue("dve", lambda e, d=d, to=to: e.scalar_tensor_tensor(out=t32[:, to, :], in0=xnew[:, d, :],
                                                                             scalar=cols[:, C_FW + d:C_FW + d + 1], in1=t32[:, 11, :],
                                                                             op0=ALU.mult, op1=ALU.mult),
                         reads=[B_y[2 * d], B_y[2 * d + 1], B_cols, B_t32[11]], writes=[B_t32[to]])
                fw.issue("sp", lambda e, d=d, to=to: e.dma_start(out=out_d[ps - 4][:, d * 512:(d + 1) * 512], in_=t32[:, to, :]),
                         reads=[B_t32[to]], dma_key=k_out[to])

        try:
            _cp(1)
            for u in ht_units(0):
                u()
            for ps in range(8):
                do_pass(ps)
                _cp(ps * 10 + 9)
        except _Stop:
            pass
        for to in (7, 8, 9):
            fw.final_wait("sp", k_out[to])
        if dry:
            return order
        assert DEBUG_STOP is not None or wstate["next_use"] == len(plan)
        fw.emit()
    return nc


def _blk(W, c0, n=128):
    w = W[:, c0:c0 + n].reshape(16, 128, n).transpose(1, 0, 2)
    return np.ascontiguousarray(w).reshape(128, 16 * n)


def _col(v):
    v = np.asarray(v, np.float32)
    return v.reshape(-1, 128).T


def prep_shared(inp):
    w_in = np.asarray(inp["w_in"][0], np.float32)
    w_br = np.asarray(inp["w_branch"][0], np.float32)
    w_out = np.asarray(inp["w_out"][0], np.float32)
    wn = np.empty((NBLK, 128, 2048), np.float32)
    for g in range(8):
        wn[ssd_blk(g, 0)] = _blk(w_in, O3 + (2 * g) * 128)
        wn[ssd_blk(g, 1)] = _blk(w_in, O3 + (2 * g + 1) * 128)
        wn[ssd_blk(g, 2)] = _blk(w_in, O4 + (2 * g) * 128)
        wn[ssd_blk(g, 3)] = _blk(w_in, O4 + (2 * g + 1) * 128)
        wn[ssd_blk(g, 4)] = _blk(w_in, OB + g * 128)
        wn[ssd_blk(g, 5)] = _blk(w_in, OC + g * 128)
        wn[gm_blk(g, 0)] = _blk(w_in, 0 + (2 * g) * 128)
        wn[gm_blk(g, 1)] = _blk(w_in, 0 + (2 * g + 1) * 128)
        wn[gm_blk(g, 2)] = _blk(w_in, O2 + (2 * g) * 128)
        wn[gm_blk(g, 3)] = _blk(w_in, O2 + (2 * g + 1) * 128)
    for d in range(16):
        wn[br_blk(d, 0)] = _blk(w_in, O6 + d * 128)
        wn[br_blk(d, 1)] = _blk(w_in, O6 + 2048 + d * 128)
        wn[br_blk(d, 2)] = _blk(w_br[0], d * 128)
        wn[br_blk(d, 3)] = _blk(w_br[1], d * 128)
        wn[out_blk(d)] = _blk(w_out, d * 128)
    wv = np.empty((4, 128, 8192), np.float32)
    for j in range(4):
        wv[j] = _blk(w_in, O1 + j * 512, 512)
    wdt = _blk(w_in, O5, 32)
    cols = np.zeros((128, NCOL), np.float32)
    cols[:, C_NW:C_NW + 16] = _col(inp["norm_w"][0])
    cols[:, C_LNW:C_LNW + 16] = _col(inp["ln_v_w"][0])
    cols[:, C_LNB:C_LNB + 16] = _col(inp["ln_v_b"][0])
    cw = np.asarray(inp["conv_w"][0], np.float32)
    for k in range(4):
        cols[:, C_CW + 32 * k:C_CW + 32 * k + 32] = _col(cw[k])
    cols[:, C_CB:C_CB + 32] = _col(inp["conv_b"][0])
    cols[:, C_SNW:C_SNW + 16] = _col(inp["ssm_norm_w"][0])
    bg = np.asarray(inp["b_gate"][0], np.float32)
    cols[:, C_BG:C_BG + 16] = _col(bg[0])
    cols[:, C_BG + 16:C_BG + 32] = _col(bg[1])
    cols[:, C_FW:C_FW + 16] = _col(inp["final_norm_w"])
    cols[:, C_DS:C_DS + 16] = _col(np.repeat(np.asarray(inp["d_skip"][0], np.float32), 64))
    rows = np.zeros((128, 64), np.float32)
    rows[:, 0:32] = np.asarray(inp["dt_bias"][0], np.float32)[None, :]
    rows[:, 32:64] = np.asarray(inp["a_log"][0], np.float32)[None, :]
    bsb = np.ascontiguousarray(np.broadcast_to(np.asarray(inp["b_spatial"][0], np.float32).reshape(1, 1024), (128, 1024)))
    ws = np.asarray(inp["w_spatial"][0], np.float32)
    wst = np.ascontiguousarray(ws.transpose(2, 0, 1)).reshape(128, 1024)
    return dict(wn=wn, wv=wv, wdt=np.ascontiguousarray(wdt), cols=cols, rows=rows, bsb=bsb, wst=wst)


def _xtiles(xb, t0):
    seg = xb[t0:t0 + 2048]
    a = seg.reshape(4, 512, 16, 128).transpose(0, 3, 2, 1)
    return np.ascontiguousarray(a).reshape(4, 128, 8192)


_NC_CACHE = {}


def kernel(**inputs):
    x = np.asarray(inputs["x"], np.float32)
    shared = prep_shared(inputs)
    if "nc" not in _NC_CACHE:
        _NC_CACHE["nc"] = build_program()
    nc = _NC_CACHE["nc"]
    in_maps = []
    for c in range(8):
        b, half = c // 2, c % 2
        xa = np.concatenate([_xtiles(x[b], 0), _xtiles(x[b], half * 2048)], axis=0)
        m = dict(shared)
        cols = shared["cols"].copy()
        cols[:, C_FLAG] = float(half)
        m["cols"] = cols
        m["x_all"] = xa
        in_maps.append(m)
    res = run_bass_kernel_spmd(nc, in_maps, core_ids=list(range(8)))
    out = np.empty((4, 4096, 2048), np.float32)
    for c in range(8):
        b, half = c // 2, c % 2
        o = np.asarray(res.results[c]["out"]).reshape(4, 128, 16, 512)
        o = o.transpose(0, 3, 2, 1).reshape(2048, 2048)
        out[b, half * 2048:(half + 1) * 2048] = o
    return out
```
